# Optimizing a Trainium2 kernel written in Bass

```python
import math
import jax, jax.numpy as jnp
from jax import lax
import numpy as np

D_MODEL = 1024
BATCH = 2
SEQ = 8192
DEPTH = 4

N_MIXERS = 2
CONV_CHANNELS = D_MODEL
CONV_WIDTH = 31
HEAD_DIM = 64
HEADS_PER_GROUP = 4
DILATION_PAIRS = ((128, 1), (512, 4), (2048, 16))
N_GROUPS = len(DILATION_PAIRS)
N_HEADS = N_GROUPS * HEADS_PER_GROUP
D_ATTN = N_HEADS * HEAD_DIM
N_BUCKETS = 32
REL_MAX_DISTANCE = 2048
D_FF = -(-8 * D_MODEL // (3 * 256)) * 256
EPS = 1e-6
N_CONV_LAYERS = (DEPTH + 1) // 2
N_ATTN_LAYERS = DEPTH // 2
NEG_INF = -1e30

kernel_name = "hybrid_conv_dilated_attn_trunk"


def rmsnorm(x, g):
    x32 = x.astype(jnp.float32)
    y = x32 * lax.rsqrt(jnp.mean(x32 * x32, axis=-1, keepdims=True) + EPS)
    return (y * g.astype(jnp.float32)).astype(x.dtype)


def layernorm(x, g, b):
    x32 = x.astype(jnp.float32)
    mu = jnp.mean(x32, axis=-1, keepdims=True)
    xc = x32 - mu
    var = jnp.mean(xc * xc, axis=-1, keepdims=True)
    y = xc * lax.rsqrt(var + EPS) * g.astype(jnp.float32) + b.astype(jnp.float32)
    return y.astype(x.dtype)


def t5_bucket(dist):
    max_exact = N_BUCKETS // 2
    n = jnp.maximum(dist, 0)
    nf = jnp.maximum(n, 1).astype(jnp.float32)
    large = max_exact + (jnp.log(nf / max_exact) / math.log(REL_MAX_DISTANCE / max_exact)
                         * (N_BUCKETS - max_exact)).astype(jnp.int32)
    large = jnp.minimum(large, N_BUCKETS - 1)
    return jnp.where(n < max_exact, n, large)


def conformer_conv(h, w_pw1, b_pw1, w_dw, b_dw, ln_g, ln_b, w_pw2, b_pw2):
    u = h @ w_pw1 + b_pw1
    a, gate = jnp.split(u, 2, axis=-1)
    u = a * jax.nn.sigmoid(gate)
    u = lax.conv_general_dilated(
        u, w_dw[:, None, :], window_strides=(1,), padding=[(CONV_WIDTH - 1, 0)],
        dimension_numbers=("NWC", "WIO", "NWC"), feature_group_count=CONV_CHANNELS) + b_dw
    u = jax.nn.silu(layernorm(u, ln_g, ln_b))
    return u @ w_pw2 + b_pw2


def dilated_group(q, k, v, bias_table, window, dilation):
    b_, s, h, hd = q.shape
    n_back = window // dilation
    seg = n_back * dilation
    s_pad = -(-s // seg) * seg
    nb = s_pad // seg
    pad = ((0, 0), (0, s_pad - s), (0, 0), (0, 0))

    def blocks(t):
        return jnp.pad(t, pad).reshape(b_, nb, n_back, dilation, h, hd)

    def with_prev(t):
        prev = jnp.pad(t[:, :-1], ((0, 0), (1, 0), (0, 0), (0, 0), (0, 0), (0, 0)))
        return jnp.concatenate([prev, t], axis=2)

    qb = blocks(q)
    kk = with_prev(blocks(k))
    vv = with_prev(blocks(v))

    i_idx = jnp.arange(n_back)[:, None]
    j_idx = jnp.arange(2 * n_back)[None, :]
    dist = i_idx + n_back - j_idx
    bias = jnp.transpose(bias_table[t5_bucket(dist * dilation)], (2, 0, 1)).astype(jnp.float32)
    valid = (dist >= 0) & (dist <= n_back)
    not_before_start = (jnp.arange(nb)[:, None, None] > 0) | (j_idx[None] >= n_back)
    mask = valid[None] & not_before_start

    logits = jnp.einsum("bnidhc,bnjdhc->bndhij", qb, kk) * (HEAD_DIM ** -0.5) + bias
    logits = jnp.where(mask[None, :, None, None], logits, NEG_INF)
    m = jnp.max(logits, axis=-1, keepdims=True)
    p = jnp.exp(logits - m)
    den = jnp.sum(p, axis=-1)
    o = jnp.einsum("bndhij,bnjdhc->bnidhc", p, vv)
    den_t = jnp.transpose(den, (0, 1, 4, 2, 3))
    lse_t = jnp.transpose(m[..., 0] + jnp.log(den), (0, 1, 4, 2, 3))
    o = (o / den_t[..., None]).reshape(b_, s_pad, h, hd)[:, :s]
    lse = lse_t.reshape(b_, s_pad, h)[:, :s]
    return o, lse


def dilated_attention(h, w_qkv, w_o, rel_bias):
    b_, s, _ = h.shape
    qkv = (h @ w_qkv).astype(jnp.float32).reshape(b_, s, 3, N_HEADS, HEAD_DIM)
    outs, lses = [], []
    for g, (window, dilation) in enumerate(DILATION_PAIRS):
        hs = slice(g * HEADS_PER_GROUP, (g + 1) * HEADS_PER_GROUP)
        o, l = dilated_group(qkv[:, :, 0, hs], qkv[:, :, 1, hs], qkv[:, :, 2, hs],
                             rel_bias[:, hs], window, dilation)
        outs.append(o)
        lses.append(l)
    alpha = jax.nn.softmax(jnp.stack(lses, axis=0), axis=0)
    o = jnp.concatenate([outs[g] * alpha[g][..., None] for g in range(N_GROUPS)], axis=2)
    return o.reshape(b_, s, D_ATTN).astype(h.dtype) @ w_o


def swiglu(h, w_gate, w_up, w_down):
    return (jax.nn.silu(h @ w_gate) * (h @ w_up)) @ w_down


def setup_inputs(seed: int = 0) -> dict:
    key = jax.random.key(seed)
    ks = jax.random.split(key, 20)
    f32 = jnp.float32

    def nrm(k, shape, scale):
        return jax.random.normal(k, shape, f32) * scale

    return {
        "x": nrm(ks[0], (BATCH, SEQ, D_MODEL), 1.0),
        "norm_mix": 1.0 + nrm(ks[1], (DEPTH, D_MODEL), 0.05),
        "norm_ffn": 1.0 + nrm(ks[2], (DEPTH, D_MODEL), 0.05),
        "final_norm": 1.0 + nrm(ks[3], (D_MODEL,), 0.05),
        "conv_w_pw1": nrm(ks[4], (N_CONV_LAYERS, D_MODEL, 2 * CONV_CHANNELS), D_MODEL ** -0.5),
        "conv_b_pw1": nrm(ks[5], (N_CONV_LAYERS, 2 * CONV_CHANNELS), 0.01),
        "conv_w_dw": nrm(ks[6], (N_CONV_LAYERS, CONV_WIDTH, CONV_CHANNELS), CONV_WIDTH ** -0.5),
        "conv_b_dw": nrm(ks[7], (N_CONV_LAYERS, CONV_CHANNELS), 0.01),
        "conv_ln_g": 1.0 + nrm(ks[8], (N_CONV_LAYERS, CONV_CHANNELS), 0.05),
        "conv_ln_b": nrm(ks[9], (N_CONV_LAYERS, CONV_CHANNELS), 0.01),
        "conv_w_pw2": nrm(ks[10], (N_CONV_LAYERS, CONV_CHANNELS, D_MODEL), CONV_CHANNELS ** -0.5),
        "conv_b_pw2": nrm(ks[11], (N_CONV_LAYERS, D_MODEL), 0.01),
        "attn_w_qkv": nrm(ks[12], (N_ATTN_LAYERS, D_MODEL, 3 * D_ATTN), D_MODEL ** -0.5),
        "attn_w_o": nrm(ks[13], (N_ATTN_LAYERS, D_ATTN, D_MODEL), D_ATTN ** -0.5),
        "rel_bias": nrm(ks[14], (N_BUCKETS, N_HEADS), 0.5),
        "ffn_w_gate": nrm(ks[15], (DEPTH, D_MODEL, D_FF), D_MODEL ** -0.5),
        "ffn_w_up": nrm(ks[16], (DEPTH, D_MODEL, D_FF), D_MODEL ** -0.5),
        "ffn_w_down": nrm(ks[17], (DEPTH, D_FF, D_MODEL), D_FF ** -0.5),
    }


def reference(x, norm_mix, norm_ffn, final_norm, conv_w_pw1, conv_b_pw1, conv_w_dw, conv_b_dw,
              conv_ln_g, conv_ln_b, conv_w_pw2, conv_b_pw2, attn_w_qkv, attn_w_o, rel_bias,
              ffn_w_gate, ffn_w_up, ffn_w_down):
    for i in range(DEPTH):
        h = rmsnorm(x, norm_mix[i])
        j = i // N_MIXERS
        if i % N_MIXERS == 0:
            x = x + conformer_conv(h, conv_w_pw1[j], conv_b_pw1[j], conv_w_dw[j], conv_b_dw[j],
                                   conv_ln_g[j], conv_ln_b[j], conv_w_pw2[j], conv_b_pw2[j])
        else:
            x = x + dilated_attention(h, attn_w_qkv[j], attn_w_o[j], rel_bias)
        h = rmsnorm(x, norm_ffn[i])
        x = x + swiglu(h, ffn_w_gate[i], ffn_w_up[i], ffn_w_down[i])
    return rmsnorm(x, final_norm)
```

```python
import numpy as np
import ml_dtypes
import concourse.bass as bass
import concourse.mybir as mybir
from concourse.bass_utils import run_bass_kernel_spmd

F32 = mybir.dt.float32
BF16 = mybir.dt.bfloat16
AF = mybir.ActivationFunctionType
ALU = mybir.AluOpType

NCORES = 8
P = 128
T = 2048
TT = 512
NT = T // TT
D = 1024
KC = D // P
DFF = 2816
FC = DFF // P
DEPTH = 4
EPS = 1e-6
NEG = -1e30
CW = 31
HALO = 32
DATT = 768
NH = 12

GRAN = 256


class Sem:
    def __init__(self, handle):
        self.h = handle
        self.count = 0


class Op:
    __slots__ = ("eng", "fn", "deps", "needed", "idx", "sem", "val", "inc", "pos")

    def __init__(self, eng, fn):
        self.eng = eng
        self.fn = fn
        self.deps = []
        self.needed = False
        self.idx = None
        self.sem = None
        self.val = None
        self.inc = 16
        self.pos = None


class Ref:
    __slots__ = ("ap", "space", "gr")

    def __init__(self, ap, space, gr):
        self.ap = ap
        self.space = space
        self.gr = gr


class Sched:
    ENGS = ("pe", "act", "dve", "pool", "sp")

    def __init__(self, nc):
        self.nc = nc
        self.q = {e: [] for e in self.ENGS}
        self.mem = {}
        self.nops = 0

    def _key(self, op):
        return op.sem if op.sem is not None else op.eng

    def op(self, eng, fn, reads=(), writes=(), dma_sem=None, inc=16, nowaw=False, extra=()):
        o = Op(eng, fn)
        o.pos = self.nops
        self.nops += 1
        if dma_sem is not None:
            o.sem = dma_sem
            o.inc = inc
            dma_sem.count += inc
            o.val = dma_sem.count
        deps = {}
        mykey = self._key(o)

        def add(d):
            k = self._key(d)
            if d.sem is None and d.eng == eng and o.sem is None:
                pass
            cur = deps.get(k)
            if cur is None or d.pos > cur.pos:
                deps[k] = d

        for r in reads:
            for g in r.gr:
                st = self.mem.get((r.space, g))
                if st is None:
                    st = self.mem[(r.space, g)] = [{}, {}]
                for d in st[0].values():
                    add(d)
                st[1][mykey] = o
        for w in writes:
            for g in w.gr:
                st = self.mem.get((w.space, g))
                if st is None:
                    st = self.mem[(w.space, g)] = [{}, {}]
                for d in st[1].values():
                    if d is o:
                        continue
                    if d.sem is None and o.sem is None and d.eng == eng and eng == "pe":
                        continue
                    add(d)
                for d in st[0].values():
                    if d is o:
                        continue
                    if d.sem is None and o.sem is None and d.eng == eng and eng == "pe":
                        continue
                    if nowaw:
                        continue
                    add(d)
                if not nowaw:
                    st[0] = {mykey: o}
                    st[1] = {}
                else:
                    st[0][mykey] = o
        for d in extra:
            add(d)
        for d in deps.values():
            if d is o:
                continue
            d.needed = True
            snap = d.sem.count if d.sem is not None else None
            if d.sem is not None and d.sem is o.sem:
                snap = d.val
            o.deps.append((d, snap))
        self.q[eng].append(o)
        return o

    def I(self, eng, method, reads=(), writes=(), dma_sem=None, inc=16, nowaw=False, extra=(), **kw):
        return self.op(eng, lambda e: getattr(e, method)(**kw), reads=reads, writes=writes,
                       dma_sem=dma_sem, inc=inc, nowaw=nowaw, extra=extra)

    def emit(self, engines, sems):
        for e in self.ENGS:
            n = 0
            for o in self.q[e]:
                if o.sem is None and o.needed:
                    n += 1
                    o.idx = n
        for e in self.ENGS:
            self._emit_eng(e, engines[e], sems)

    def _emit_eng(self, e, eng, sems):
        seen = {}
        for o in self.q[e]:
            waits = {}
            for (d, snap) in o.deps:
                if d.sem is not None:
                    k, h, v = id(d.sem), d.sem.h, snap
                else:
                    k, h, v = d.eng, sems[d.eng].h, d.idx
                if seen.get(k, 0) >= v:
                    continue
                if k not in waits or waits[k][1] < v:
                    waits[k] = (h, v)
            for k, (h, v) in waits.items():
                eng.wait_ge(h, v)
                seen[k] = v
            if o.fn is None:
                continue
            ins = o.fn(eng)
            if o.sem is not None:
                ins.then_inc(o.sem.h, o.inc)
            elif o.needed:
                ins.then_inc(sems[e].h, 1)


class Mem:
    def __init__(self, space, base_ap_by_dtype, gran=GRAN):
        self.space = space
        self.gran = gran
        self.base = base_ap_by_dtype
        self.off = 0

    def alloc(self, nbytes, align=GRAN):
        self.off = (self.off + align - 1) // align * align
        o = self.off
        self.off += nbytes
        return o


class Tn:
    def __init__(self, mem, off, dtype, shape):
        self.mem = mem
        self.off = off
        self.dtype = dtype
        self.esz = 4 if dtype == F32 else 2
        self.shape = tuple(shape)
        n = int(np.prod(shape))
        self.nbytes = n * self.esz
        e0 = off // self.esz
        ap = mem.base[dtype][:, e0:e0 + n]
        if len(shape) > 1:
            names = " ".join("d%d" % i for i in range(len(shape)))
            kw = {"d%d" % i: shape[i] for i in range(1, len(shape))}
            ap = ap.rearrange("p (%s) -> p %s" % (names, names), **kw)
        self.ap = ap
        st = [self.esz]
        for s in reversed(self.shape[1:]):
            st.insert(0, st[0] * s)
        self.strides = st

    def __getitem__(self, idx):
        if not isinstance(idx, tuple):
            idx = (idx,)
        idx = idx + (slice(None),) * (len(self.shape) - len(idx))
        ap = self.ap[(slice(None),) + idx]
        return Ref(ap, self.mem.space, self.granules(idx))

    def granules(self, idx):
        rng = []
        for i, s in zip(idx, self.shape):
            if isinstance(i, int):
                rng.append((i, 1, 1))
            else:
                a, b, c = i.indices(s)
                rng.append((a, (b - a + c - 1) // c, c))
        out = set()
        inner = rng[-1]
        outer = rng[:-1]

        def rec(d, base):
            if d == len(outer):
                a, n, c = inner
                lo = base + a * self.strides[-1]
                hi = base + (a + (n - 1) * c + 1) * self.strides[-1]
                gsz = self.mem.gran
                for g in range(lo // gsz, (hi - 1) // gsz + 1):
                    out.add(g)
                return
            a, n, c = outer[d]
            for j in range(n):
                rec(d + 1, base + (a + j * c) * self.strides[d])

        rec(0, self.off)
        return out

    def whole(self):
        return self[tuple(slice(None) for _ in self.shape)]

    def pslice(self, p0, p1, idx):
        r = self[idx]
        return Ref(r.ap[p0:p1], r.space, r.gr)


def dref(ap, name, ids):
    return Ref(ap, "dram:" + name, set(ids))


class CstLayout:
    def __init__(self):
        self.cols = {}
        self.n = 0

    def add(self, name, ncols):
        self.cols[name] = (self.n, ncols)
        self.n += ncols

    def sl(self, name):
        a, n = self.cols[name]
        return a, a + n


def cst_layout():
    L = CstLayout()
    L.add("norm_mix", DEPTH * KC)
    L.add("norm_ffn", DEPTH * KC)
    L.add("final_norm", KC)
    L.add("b_pw1", 2 * 16)
    L.add("w_dw", 2 * KC * CW)
    L.add("b_dw", 2 * KC)
    L.add("ln_g", 2 * KC)
    L.add("ln_b", 2 * KC)
    L.add("b_pw2", 2 * KC)
    L.add("hv", 1)
    L.add("hb", 1)
    L.n = (L.n + 63) // 64 * 64
    return L


CL = cst_layout()


def colmajor(v, nch):
    v = np.asarray(v, np.float32)
    lead = v.shape[:-1]
    v = v.reshape(lead + (nch, P))
    v = np.moveaxis(v, -1, 0)
    return np.ascontiguousarray(v).reshape(P, -1)


class Builder:
    def __init__(self, stages, final=True):
        self.stages = stages
        self.final = final
        nc = bass.Bass("TRN2", target_bir_lowering=False)
        self.nc = nc
        self.S = Sched(nc)

    def declare(self):
        nc = self.nc
        self.xT = nc.dram_tensor("xT", [P, KC, T], F32, kind="ExternalInput").ap()
        self.yT = nc.dram_tensor("yT", [P, KC, T], F32, kind="ExternalOutput").ap()
        self.cst = nc.dram_tensor("cst", [P, CL.n], F32, kind="ExternalInput").ap()
        self.wffn = nc.dram_tensor("wffn", [DEPTH, 3, FC, P, 1024], F32, kind="ExternalInput").ap()
        self.wconv = nc.dram_tensor("wconv", [2, 24, P, 1024], F32, kind="ExternalInput").ap()
        self.identd = nc.dram_tensor("identd", [P, P], F32, kind="ExternalInput").ap()
        self.wattn = nc.dram_tensor("wattn", [2, 24, P, 1024], F32, kind="ExternalInput").ap()
        self.bmask = nc.dram_tensor("bmask", [P, NH * 256], F32, kind="ExternalInput").ap()
        kinds = [k for (k, _) in self.stages]
        if "conv_in" in kinds:
            self.xh_in = nc.dram_tensor("xh_in", [P, KC, HALO], F32, kind="ExternalInput").ap()
        if "attn_in" in kinds:
            self.halo_in = nc.dram_tensor("halo_in", [P, 84, P], BF16, kind="ExternalInput").ap()
        if "attn_prod" in kinds:
            self.halo_out = nc.dram_tensor("halo_out", [P, 84, P], BF16, kind="ExternalOutput").ap()

    def build(self):
        nc = self.nc
        self.declare()
        SB_BYTES = 207 * 1024
        with (
            nc.sbuf_tensor("sb16", [P, SB_BYTES // 2], BF16) as sb16,
            nc.psum_tensor("ps32", [P, 4096], F32) as ps32,
            nc.semaphore("s_pe") as s_pe, nc.semaphore("s_act") as s_act,
            nc.semaphore("s_dve") as s_dve, nc.semaphore("s_pool") as s_pool,
            nc.semaphore("s_sp") as s_sp,
        ):
            self.sems = {"pe": Sem(s_pe), "act": Sem(s_act), "dve": Sem(s_dve),
                         "pool": Sem(s_pool), "sp": Sem(s_sp)}
            sb32 = sb16[:, :].bitcast(F32)
            ps16 = ps32[:, :].bitcast(BF16)
            self.sb = Mem("sb", {BF16: sb16[:, :], F32: sb32})
            self.ps = Mem("ps", {F32: ps32[:, :], BF16: ps16}, gran=2048)
            self.SB_BYTES = SB_BYTES
            self._sem_cms = []
            try:
                self.program()
                with nc.Block() as block:
                    engines = {}

                    def run(name):
                        def f(eng):
                            self.S._emit_eng(name, eng, self.sems)
                        return f
                    for e in Sched.ENGS:
                        n = 0
                        for o in self.S.q[e]:
                            if o.sem is None and o.needed:
                                n += 1
                                o.idx = n
                    block.tensor(run("pe"))
                    block.scalar(run("act"))
                    block.vector(run("dve"))
                    block.gpsimd(run("pool"))
                    block.sync(run("sp"))
            finally:
                for cm in reversed(self._sem_cms):
                    cm.__exit__(None, None, None)
        return nc

    def prev_rank(self):
        if getattr(self, "_prev", None) is None:
            pid = self.nc.partition_id()
            self._prev = self.nc.gpsimd.snap((pid + (NCORES - 1)) % NCORES)
        return self._prev

    def new_sem(self, name):
        cm = self.nc.semaphore(name)
        h = cm.__enter__()
        self._sem_cms.append(cm)
        return Sem(h)

    def sbt(self, off, dtype, shape):
        return Tn(self.sb, off, dtype, shape)

    def psbank(self, b, dtype=F32, shape=None):
        if shape is None:
            shape = (512,) if dtype == F32 else (1024,)
        return Tn(self.ps, b * 2048, dtype, shape)

    def program(self):
        S = self.S
        sb = self.sb
        o_cst = sb.alloc(CL.n * 4)
        self.CST = self.sbt(o_cst, F32, (CL.n,))
        o_id = sb.alloc(256)
        self.ONES = self.sbt(o_id, BF16, (128,))
        o_id2 = sb.alloc(256)
        self.IDENT = self.sbt(o_id2, BF16, (128,))
        o_x = sb.alloc(KC * T * 4)
        self.X = self.sbt(o_x, F32, (KC, T))
        self.NSLOT = 16
        o_ring = sb.alloc(self.NSLOT * 2048)
        self.RING = [self.sbt(o_ring + i * 2048, BF16, (1024,)) for i in range(self.NSLOT)]
        self.ring_sems = [self.new_sem("ring%d" % i) for i in range(self.NSLOT)]
        self.ring_next = 0
        self.scr0 = sb.alloc(0)
        self.scr_bytes = self.SB_BYTES - self.scr0
        self.psn = 0
        self.held = set()
        self.io_sem = self.new_sem("io")
        self.tail_ops = []
        self.out_sem = self.new_sem("outs")

        S.op("sp", lambda e: e.dma_start(out=self.CST.whole().ap, in_=self.cst[:, :]),
             writes=[self.CST.whole()], dma_sem=self.io_sem)
        S.op("pool", lambda e: e.memset(self.ONES.whole().ap, 1.0), writes=[self.ONES.whole()])
        S.op("pool", lambda e: e.dma_start(out=self.IDENT.whole().ap, in_=self.identd[:, :]),
             writes=[self.IDENT.whole()], dma_sem=self.new_sem("identsem"))
        for k in range(KC):
            S.op("sp", (lambda k: lambda e: e.dma_start(out=self.X[k].ap, in_=self.xT[:, k, :]))(k),
                 writes=[self.X[k]], dma_sem=self.new_sem("xin%d" % k))

        for st in self.stages:
            kind, li = st
            if kind == "ffn":
                self.ffn(li)
            elif kind == "conv":
                self.conv(li)
            elif kind == "conv_in":
                self.conv(li, "in")
            elif kind == "attn":
                self.attn(li, "x")
            elif kind == "attn_in":
                self.attn(li, "in")
            elif kind == "attn_prod":
                self.attn(li, "prod")
        self.final_out()

    def cst_col(self, name, j):
        a, _ = CL.sl(name)
        return self.CST[a + j:a + j + 1]

    def next_bank(self):
        while True:
            b = self.psn % 8
            self.psn += 1
            if b not in self.held:
                return b

    def load_piece(self, src_ap):
        s = self.ring_next % self.NSLOT
        self.ring_next += 1
        slot = self.RING[s]
        self.S.op("pool", lambda e: e.dma_start(out=slot.whole().ap, in_=src_ap),
                  writes=[slot.whole()], dma_sem=self.ring_sems[s])
        return slot

    def rmsnorm_tile(self, tt, gname, gidx, H, SQ, RSA, RS, out_f32=None):
        ts = slice(tt * TT, (tt + 1) * TT)
        src = lambda k: self.X[k, ts]
        if out_f32 is None:
            dst = lambda k: H[k, ts]
        else:
            dst = lambda k: out_f32[k]
        self.rmsnorm_gen(src, self.X[:, ts], TT, gname, gidx, dst, SQ, RSA, RS)

    def rmsnorm_gen(self, src, src_all, n, gname, gidx, dst, SQ, RSA, RS):
        S = self.S
        sq_all = SQ[:, 0:n]
        S.op("act", lambda e: e.activation(out=sq_all.ap, in_=src_all.ap, func=AF.Square),
             reads=[src_all], writes=[sq_all])
        b = self.next_bank()
        pb = self.psbank(b)[0:n]
        for k in range(KC):
            S.op("pe", (lambda k: lambda e: e.matmul(pb.ap, lhsT=self.ONES.whole().ap, rhs=SQ[k, 0:n].ap,
                                                     start=(k == 0), stop=(k == KC - 1)))(k),
                 reads=[self.ONES.whole(), SQ[k, 0:n]], writes=[pb])
        rsa = RSA[0:n]
        rs = RS[0:n]
        S.op("act", lambda e: e.activation(out=rsa.ap, in_=pb.ap, func=AF.Sqrt,
                                           bias=self.EPSC.whole().ap, scale=1.0 / D),
             reads=[pb, self.EPSC.whole()], writes=[rsa])
        S.op("dve", lambda e: e.reciprocal(out=rs.ap, in_=rsa.ap), reads=[rsa], writes=[rs])
        for k in range(KC):
            g = self.cst_col(gname, gidx * KC + k)
            S.op("dve", (lambda k, g: lambda e: e.scalar_tensor_tensor(
                out=dst(k).ap, in0=src(k).ap, scalar=g.ap, in1=rs.ap,
                op0=ALU.mult, op1=ALU.mult))(k, g),
                reads=[src(k), g, rs], writes=[dst(k)])

    def scratch(self):
        return [self.scr0]

    def salloc(self, cur, nbytes):
        o = (cur[0] + GRAN - 1) // GRAN * GRAN
        cur[0] = o + nbytes
        assert cur[0] <= self.SB_BYTES, ("scratch overflow", cur[0], self.SB_BYTES)
        return o

    def ensure_eps(self):
        if hasattr(self, "EPSC"):
            return
        o = self.SB_BYTES - GRAN
        self.SB_BYTES -= GRAN
        self.EPSC = self.sbt(o, F32, (1,))
        self.S.op("pool", lambda e: e.memset(self.EPSC.whole().ap, EPS), writes=[self.EPSC.whole()])

    def ffn(self, li):
        S = self.S
        self.ensure_eps()
        cur = self.scratch()
        H = self.sbt(self.salloc(cur, KC * T * 2), BF16, (KC, T))
        NH_F = FC // 2
        A = self.sbt(self.salloc(cur, NH_F * T * 2), BF16, (NH_F, T))
        SQ = self.sbt(self.salloc(cur, KC * TT * 2), BF16, (KC, TT))
        RSA = [self.sbt(self.salloc(cur, TT * 4), F32, (TT,)) for _ in range(2)]
        RS = [self.sbt(self.salloc(cur, TT * 4), F32, (TT,)) for _ in range(2)]
        SG = [self.sbt(self.salloc(cur, TT * 4), F32, (TT,)) for _ in range(3)]
        for tt in range(NT):
            self.rmsnorm_tile(tt, "norm_ffn", li, H, SQ, RSA[tt % 2], RS[tt % 2])
        sgi = 0
        for hf in range(2):
            fs = list(range(hf * NH_F, (hf + 1) * NH_F))
            wd_slots = {}
            for fi, f in enumerate(fs):
                wg = self.load_piece(self.wffn[li, 0, f])
                wu = self.load_piece(self.wffn[li, 1, f])
                wg3 = Tn(self.sb, wg.off, BF16, (KC, P))
                wu3 = Tn(self.sb, wu.off, BF16, (KC, P))
                for tt in range(NT):
                    ts = slice(tt * TT, (tt + 1) * TT)
                    bg = self.psbank(self.next_bank())
                    bu = self.psbank(self.next_bank())
                    for (w3, pb) in ((wg3, bg), (wu3, bu)):
                        for k in range(KC):
                            S.op("pe", (lambda w3, pb, k, ts: lambda e: e.matmul(
                                pb.whole().ap, lhsT=w3[k].ap, rhs=H[k, ts].ap,
                                start=(k == 0), stop=(k == KC - 1)))(w3, pb, k, ts),
                                reads=[w3[k], H[k, ts]], writes=[pb.whole()])
                    sg = SG[sgi % 3]
                    sgi += 1
                    S.op("act", (lambda sg, bg: lambda e: e.activation(out=sg.whole().ap, in_=bg.whole().ap,
                                                                       func=AF.Silu))(sg, bg),
                         reads=[bg.whole()], writes=[sg.whole()])
                    S.op("dve", (lambda sg, bu, fi, ts: lambda e: e.tensor_tensor(
                        out=A[fi, ts].ap, in0=bu.whole().ap, in1=sg.whole().ap, op=ALU.mult))(sg, bu, fi, ts),
                        reads=[bu.whole(), sg.whole()], writes=[A[fi, ts]])
            for fi, f in enumerate(fs):
                wd_slots[fi] = self.load_piece(self.wffn[li, 2, f])
            for tt in range(NT):
                ts = slice(tt * TT, (tt + 1) * TT)
                for dc in range(KC):
                    pb = self.psbank(self.next_bank())
                    for fi in range(NH_F):
                        wd = wd_slots[fi]
                        S.op("pe", (lambda wd, pb, fi, ts, dc: lambda e: e.matmul(
                            pb.whole().ap, lhsT=wd[dc * P:(dc + 1) * P].ap, rhs=A[fi, ts].ap,
                            start=(fi == 0), stop=(fi == NH_F - 1)))(wd, pb, fi, ts, dc),
                            reads=[wd[dc * P:(dc + 1) * P], A[fi, ts]], writes=[pb.whole()])
                    S.op("dve", (lambda pb, dc, ts: lambda e: e.tensor_tensor(
                        out=self.X[dc, ts].ap, in0=pb.whole().ap, in1=self.X[dc, ts].ap, op=ALU.add))(pb, dc, ts),
                        reads=[pb.whole(), self.X[dc, ts]], writes=[self.X[dc, ts]])

    def final_out(self):
        S = self.S
        self.ensure_eps()
        cur = self.scratch()
        SQ = self.sbt(self.salloc(cur, KC * TT * 2), BF16, (KC, TT))
        RSA = [self.sbt(self.salloc(cur, TT * 4), F32, (TT,)) for _ in range(2)]
        RS = [self.sbt(self.salloc(cur, TT * 4), F32, (TT,)) for _ in range(2)]
        OUT = [self.sbt(self.salloc(cur, KC * TT * 4), F32, (KC, TT)) for _ in range(2)]
        outs = []
        for tt in range(NT):
            ts = slice(tt * TT, (tt + 1) * TT)
            if self.final:
                self.rmsnorm_tile(tt, "final_norm", 0, None, SQ, RSA[tt % 2], RS[tt % 2], out_f32=OUT[tt % 2])
                src = OUT[tt % 2].whole()
            else:
                src = self.X[:, ts]
            outs.append(S.op("sp", (lambda src, ts: lambda e: e.dma_start(out=self.yT[:, :, ts], in_=src.ap))(src, ts),
                             reads=[src], dma_sem=self.out_sem))
        S.op("sp", None, extra=outs + self.tail_ops)

    def exchange_rows(self, name, src_ref, ncols, dtype, dst_ref, k3=None):
        nc = self.nc
        S = self.S
        bnc = nc.dram_tensor("bnc_" + name, [P, ncols], dtype)
        gat = nc.dram_tensor("gat_" + name, [NCORES * P, ncols], dtype)
        sem1 = self.new_sem("x1_" + name)
        sem2 = self.new_sem("x2_" + name)
        sem3 = self.new_sem("x3_" + name)
        rsh = (lambda a: a) if k3 is None else (lambda a: a.rearrange("p (k t) -> p k t", k=k3))
        S.op("pool", lambda e: e.dma_start(out=rsh(bnc[:, :]), in_=src_ref.ap),
             reads=[src_ref], writes=[dref(None, "bnc_" + name, [0])], dma_sem=sem1)
        S.op("pool", lambda e: e.collective_compute(
            "AllGather", ALU.bypass, replica_groups=[list(range(NCORES))],
            ins=[bnc.ap().opt()], outs=[gat.ap().opt()]),
            reads=[dref(None, "bnc_" + name, [0])], writes=[dref(None, "gat_" + name, [0])],
            dma_sem=sem2, inc=1)

        def ld(e):
            prev = self.prev_rank()
            return e.dma_start(out=dst_ref.ap, in_=rsh(gat[bass.ds(prev * P, P), :]))
        S.op("pool", ld, reads=[dref(None, "gat_" + name, [0])], writes=[dst_ref], dma_sem=sem3)

    def conv(self, li, mode="x"):
        S = self.S
        self.ensure_eps()
        j = li // 2
        TH = T // 2
        cur = self.scratch()
        H = self.sbt(self.salloc(cur, KC * TH * 2), BF16, (KC, TH))
        UW = HALO + TH
        U = self.sbt(self.salloc(cur, KC * UW * 2), BF16, (KC, UW))
        V = self.sbt(self.salloc(cur, KC * TH * 4), F32, (KC, TH))
        DG = self.sbt(self.salloc(cur, CW * P * 2), BF16, (CW, P))
        SQ = self.sbt(self.salloc(cur, KC * TT * 2), BF16, (KC, TT))
        RSA = [self.sbt(self.salloc(cur, TT * 4), F32, (TT,)) for _ in range(2)]
        RS = [self.sbt(self.salloc(cur, TT * 4), F32, (TT,)) for _ in range(2)]
        SGT = [self.sbt(self.salloc(cur, TT * 4), F32, (TT,)) for _ in range(2)]
        V16 = [self.sbt(self.salloc(cur, TT * 2), BF16, (TT,)) for _ in range(2)]
        Q16 = [self.sbt(self.salloc(cur, TT * 2), BF16, (TT,)) for _ in range(2)]
        MEAN = self.sbt(self.salloc(cur, TT * 4), F32, (TT,))
        MSQ = self.sbt(self.salloc(cur, TT * 4), F32, (TT,))
        XH = self.sbt(self.salloc(cur, KC * HALO * 4), F32, (KC, HALO))
        HH = self.sbt(self.salloc(cur, KC * HALO * 2), BF16, (KC, HALO))
        wbase = lambda pc: self.wconv[j, pc]
        hv = self.cst_col("hv", 0)

        if mode == "x":
            self.exchange_rows("cv%d" % li, self.X[:, T - HALO:T], KC * HALO, F32, XH.whole(), k3=KC)
        else:
            S.I("sp", "dma_start", writes=[XH.whole()], dma_sem=self.new_sem("xh%d" % li),
                out=XH.whole().ap, in_=self.xh_in)

        cnt = 0
        import os
        stop = os.environ.get("DBG_CONV_STOP", "")
        if stop == "X":
            return
        for hh in range(2):
            t0 = hh * TH
            for t2 in range(2):
                gs = slice(t0 + t2 * TT, t0 + (t2 + 1) * TT)
                ls = slice(t2 * TT, (t2 + 1) * TT)
                self.rmsnorm_gen((lambda gs: lambda k: self.X[k, gs])(gs), self.X[:, gs], TT, "norm_mix", li,
                                 (lambda ls: lambda k: H[k, ls])(ls), SQ, RSA[t2], RS[t2])
            if hh == 0:
                self.rmsnorm_gen(lambda k: XH[k], XH.whole(), HALO, "norm_mix", li,
                                 lambda k: HH[k], SQ, RSA[0], RS[0])
            else:
                S.op("dve", lambda e: e.tensor_copy(out=U[:, 0:HALO].ap, in_=U[:, TH:TH + HALO].ap),
                     reads=[U[:, TH:TH + HALO]], writes=[U[:, 0:HALO]])
            if stop == "A":
                continue
            for co in range(KC):
                wa = self.load_piece(wbase(co))
                wg = self.load_piece(wbase(co + KC))
                wa3 = Tn(self.sb, wa.off, BF16, (KC, P))
                wg3 = Tn(self.sb, wg.off, BF16, (KC, P))
                ba = self.cst_col("b_pw1", j * 16 + co)
                bg = self.cst_col("b_pw1", j * 16 + co + KC)
                tiles = [(H, slice(t2 * TT, (t2 + 1) * TT), HALO + t2 * TT, TT) for t2 in range(2)]
                if hh == 0 and os.environ.get("DBG_B1", "") != "nohalo":
                    tiles.append((HH, slice(0, HALO), 0, HALO))
                for (src, ssl, uo, n) in tiles:
                    pa = self.psbank(self.next_bank())[0:n]
                    pg = self.psbank(self.next_bank())[0:n]
                    for (w3, pb) in ((wa3, pa), (wg3, pg)):
                        for k in range(KC):
                            S.op("pe", (lambda w3, pb, k, src, ssl: lambda e: e.matmul(
                                pb.ap, lhsT=w3[k].ap, rhs=src[k, ssl].ap,
                                start=(k == 0), stop=(k == KC - 1)))(w3, pb, k, src, ssl),
                                reads=[w3[k], src[k, ssl]], writes=[pb])
                    sg = SGT[cnt % 2][0:n]
                    cnt += 1
                    S.op("act", (lambda sg, pg, bg: lambda e: e.activation(
                        out=sg.ap, in_=pg.ap, func=AF.Sigmoid, bias=bg.ap))(sg, pg, bg),
                        reads=[pg, bg], writes=[sg])
                    if n == HALO and os.environ.get("DBG_B1", "") != "nohv":
                        S.op("dve", (lambda sg: lambda e: e.tensor_scalar(
                            out=sg.ap, in0=sg.ap, scalar1=hv.ap, scalar2=None, op0=ALU.mult))(sg),
                            reads=[sg, hv], writes=[sg])
                    ud = U[co, uo:uo + n]
                    S.op("dve", (lambda sg, pa, ba, ud: lambda e: e.scalar_tensor_tensor(
                        out=ud.ap, in0=pa.ap, scalar=ba.ap, in1=sg.ap, op0=ALU.add, op1=ALU.mult))(sg, pa, ba, ud),
                        reads=[pa, ba, sg], writes=[ud])
            if stop == "B1":
                continue
            stb_ids = [self.next_bank() for _ in range(4)]
            self.held = set(stb_ids)
            st_sum = [self.psbank(b) for b in stb_ids[0:2]]
            st_sq = [self.psbank(b) for b in stb_ids[2:4]]
            for co in range(KC):
                for k in range(CW):
                    wc = self.cst_col("w_dw", (j * KC + co) * CW + k)
                    S.op("pool", (lambda k, wc: lambda e: e.tensor_scalar(
                        out=DG[k].ap, in0=self.IDENT.whole().ap, scalar1=wc.ap, scalar2=None, op0=ALU.mult))(k, wc),
                        reads=[self.IDENT.whole(), wc], writes=[DG[k]])
                bd = self.cst_col("b_dw", j * KC + co)
                for t2 in range(2):
                    ls = slice(t2 * TT, (t2 + 1) * TT)
                    pc = self.psbank(self.next_bank())
                    for k in range(CW):
                        rr = U[co, t2 * TT + 2 + k: t2 * TT + 2 + k + TT]
                        S.op("pe", (lambda k, rr, pc: lambda e: e.matmul(
                            pc.whole().ap, lhsT=DG[k].ap, rhs=rr.ap, start=(k == 0), stop=(k == CW - 1)))(k, rr, pc),
                            reads=[DG[k], rr], writes=[pc.whole()])
                    vd = V[co, ls]
                    S.op("act", (lambda vd, pc, bd: lambda e: e.activation(
                        out=vd.ap, in_=pc.whole().ap, func=AF.Identity, bias=bd.ap))(vd, pc, bd),
                        reads=[pc.whole(), bd], writes=[vd])
                    v16 = V16[cnt % 2]
                    q16 = Q16[cnt % 2]
                    cnt += 1
                    S.op("pool", (lambda v16, vd: lambda e: e.tensor_copy(out=v16.whole().ap, in_=vd.ap))(v16, vd),
                         reads=[vd], writes=[v16.whole()])
                    S.op("pool", (lambda q16, vd: lambda e: e.tensor_tensor(
                        out=q16.whole().ap, in0=vd.ap, in1=vd.ap, op=ALU.mult))(q16, vd),
                        reads=[vd], writes=[q16.whole()])
                    for (src16, stb) in ((v16, st_sum[t2]), (q16, st_sq[t2])):
                        S.op("pe", (lambda src16, stb, co: lambda e: e.matmul(
                            stb.whole().ap, lhsT=self.ONES.whole().ap, rhs=src16.whole().ap,
                            start=(co == 0), stop=(co == KC - 1)))(src16, stb, co),
                            reads=[self.ONES.whole(), src16.whole()], writes=[stb.whole()])
            self.held = set()
            if stop == "B2":
                continue
            for t2 in range(2):
                ls = slice(t2 * TT, (t2 + 1) * TT)
                rsa = RSA[t2]
                rs = RS[t2]
                S.op("dve", (lambda stb: lambda e: e.tensor_scalar(
                    out=MEAN.whole().ap, in0=stb.whole().ap, scalar1=1.0 / D, scalar2=None, op0=ALU.mult))(st_sum[t2]),
                    reads=[st_sum[t2].whole()], writes=[MEAN.whole()])
                S.op("dve", lambda e: e.tensor_tensor(out=MSQ.whole().ap, in0=MEAN.whole().ap, in1=MEAN.whole().ap,
                                                      op=ALU.mult),
                     reads=[MEAN.whole()], writes=[MSQ.whole()])
                S.op("dve", (lambda stb: lambda e: e.scalar_tensor_tensor(
                    out=MSQ.whole().ap, in0=stb.whole().ap, scalar=1.0 / D, in1=MSQ.whole().ap,
                    op0=ALU.mult, op1=ALU.subtract))(st_sq[t2]),
                    reads=[st_sq[t2].whole(), MSQ.whole()], writes=[MSQ.whole()])
                S.op("act", (lambda rsa: lambda e: e.activation(out=rsa.whole().ap, in_=MSQ.whole().ap, func=AF.Sqrt,
                                                                bias=self.EPSC.whole().ap))(rsa),
                     reads=[MSQ.whole(), self.EPSC.whole()], writes=[rsa.whole()])
                S.op("dve", (lambda rs, rsa: lambda e: e.reciprocal(out=rs.whole().ap, in_=rsa.whole().ap))(rs, rsa),
                     reads=[rsa.whole()], writes=[rs.whole()])
                for co in range(KC):
                    vd = V[co, ls]
                    lg = self.cst_col("ln_g", j * KC + co)
                    lb = self.cst_col("ln_b", j * KC + co)
                    S.op("dve", (lambda vd: lambda e: e.tensor_tensor(
                        out=vd.ap, in0=vd.ap, in1=MEAN.whole().ap, op=ALU.subtract))(vd),
                        reads=[vd, MEAN.whole()], writes=[vd])
                    S.op("dve", (lambda vd, rs: lambda e: e.tensor_tensor(
                        out=vd.ap, in0=vd.ap, in1=rs.whole().ap, op=ALU.mult))(vd, rs),
                        reads=[vd, rs.whole()], writes=[vd])
                    S.op("act", (lambda vd, co, ls, lg, lb: lambda e: e.activation(
                        out=H[co, ls].ap, in_=vd.ap, func=AF.Silu, bias=lb.ap, scale=lg.ap))(vd, co, ls, lg, lb),
                        reads=[vd, lg, lb], writes=[H[co, ls]])
            if stop == "C":
                continue
            w2 = [self.load_piece(wbase(16 + ci)) for ci in range(KC)]
            for t2 in range(2):
                ls = slice(t2 * TT, (t2 + 1) * TT)
                gs = slice(t0 + t2 * TT, t0 + (t2 + 1) * TT)
                for dc in range(KC):
                    pb = self.psbank(self.next_bank())
                    b2 = self.cst_col("b_pw2", j * KC + dc)
                    for ci in range(KC):
                        S.op("pe", (lambda ci, wsl, pb, ls: lambda e: e.matmul(
                            pb.whole().ap, lhsT=wsl.ap, rhs=H[ci, ls].ap,
                            start=(ci == 0), stop=(ci == KC - 1)))(ci, w2[ci][dc * P:(dc + 1) * P], pb, ls),
                            reads=[w2[ci][dc * P:(dc + 1) * P], H[ci, ls]], writes=[pb.whole()])
                    S.op("dve", (lambda pb, b2, dc, gs: lambda e: e.scalar_tensor_tensor(
                        out=self.X[dc, gs].ap, in0=pb.whole().ap, scalar=b2.ap, in1=self.X[dc, gs].ap,
                        op0=ALU.add, op1=ALU.add))(pb, b2, dc, gs),
                        reads=[pb.whole(), b2, self.X[dc, gs]], writes=[self.X[dc, gs]])


    def attn(self, li, mode="x"):
        S = self.S
        nc = self.nc
        self.ensure_eps()
        j = li // 2
        DIL = (1, 4, 16)
        xo = self.X.off
        Q = self.sbt(xo, BF16, (6, T))
        K = self.sbt(xo + 6 * T * 2, BF16, (6, T))
        DEN = self.sbt(xo + 12 * T * 2, F32, (2, T))
        cur = self.scratch()
        H = self.sbt(self.salloc(cur, KC * T * 2), BF16, (KC, T))
        O = H
        VT = self.sbt(self.salloc(cur, 6 * 16 * P * 2), BF16, (6, 16, P))
        o_halo = self.salloc(cur, 84 * P * 2)
        HAL = self.sbt(o_halo, BF16, (84, P))
        SQ = self.sbt(o_halo, BF16, (KC, TT))
        RSA = [self.sbt(o_halo + 8192 + i * 2048, F32, (TT,)) for i in range(2)]
        RS = [self.sbt(o_halo + 8192 + 4096 + i * 2048, F32, (TT,)) for i in range(2)]
        BIAS = self.sbt(self.salloc(cur, NH * 256 * 4), F32, (NH, 256))
        if mode == "prod":
            K = self.sbt(o_halo, BF16, (6, T))
            assert 6 * T * 2 <= 84 * P * 2 + NH * 256 * 4
        VTMP = [self.sbt(self.salloc(cur, T * 2), BF16, (T,)) for _ in range(2)]
        STMP = [self.sbt(self.salloc(cur, TT * 4), F32, (TT,)) for _ in range(2)]
        PT = [self.sbt(self.salloc(cur, TT * 2), BF16, (TT,)) for _ in range(3)]
        hb = self.cst_col("hb", 0)
        wbase = lambda pc: self.wattn[j, pc]
        xpark = nc.dram_tensor("xpark%d" % li, [P, KC, T], F32)
        park_sems = [self.new_sem("park%d_%d" % (li, k)) for k in range(KC)]

        if mode != "prod":
            S.I("sp", "dma_start", writes=[BIAS.whole()], dma_sem=self.new_sem("bias%d" % li),
                out=BIAS.whole().ap, in_=self.bmask.rearrange("p (h c) -> p h c", h=NH))
        for tt in range(NT):
            self.rmsnorm_tile(tt, "norm_mix", li, H, SQ, RSA[tt % 2], RS[tt % 2])
        for k in range(KC):
            if mode == "prod":
                break
            S.I("sp", "dma_start", reads=[self.X[k]], writes=[dref(None, "xpark%d" % li, [k])], dma_sem=park_sems[k],
                out=xpark[:, k, :], in_=self.X[k].ap)

        import os
        astop = os.environ.get("DBG_ATTN_STOP", "")
        if astop == "park":
            return
        evi = 0

        def project(oc, dst_row, scale):
            nonlocal evi
            g = (oc % 6) // 2
            d = DIL[g]
            w = self.load_piece(wbase(oc))
            w3 = Tn(self.sb, w.off, BF16, (KC, P))
            dst3 = Tn(self.sb, dst_row.off, BF16, (d, T // d))
            for tt in range(NT):
                ts = slice(tt * TT, (tt + 1) * TT)
                pb = self.psbank(self.next_bank())
                for k in range(KC):
                    S.I("pe", "matmul", reads=[w3[k], H[k, ts]], writes=[pb.whole()],
                        out=pb.whole().ap, lhsT=w3[k].ap, rhs=H[k, ts].ap, start=(k == 0), stop=(k == KC - 1))
                n = TT // d
                dref_ = dst3[:, tt * n:(tt + 1) * n]
                src_ap = pb.whole().ap
                out_ap = dref_.ap
                if d > 1:
                    src_ap = src_ap.rearrange("p (j r) -> p r j", r=d)
                else:
                    out_ap = dst_row[tt * TT:(tt + 1) * TT].ap
                if evi % 2 == 0:
                    S.I("act", "activation", reads=[pb.whole()], writes=[dref_], nowaw=True,
                        out=out_ap, in_=src_ap, func=AF.Copy, scale=scale)
                else:
                    S.I("dve", "tensor_scalar", reads=[pb.whole()], writes=[dref_], nowaw=True,
                        out=out_ap, in0=src_ap, scalar1=scale, scalar2=None, op0=ALU.mult)
                evi += 1

        tbank = None
        for vc in range(6):
            vt = VTMP[vc % 2]
            project(12 + vc, vt, 1.0)
            for u0 in (0, 8):
                pb = self.psbank(self.next_bank(), BF16)
                for i in range(8):
                    u = u0 + i
                    S.I("pe", "transpose", reads=[vt[u * P:(u + 1) * P], self.IDENT.whole()],
                        writes=[pb[i * P:(i + 1) * P]],
                        out=pb[i * P:(i + 1) * P].ap, in_=vt[u * P:(u + 1) * P].ap, identity=self.IDENT.whole().ap)
                S.I("dve" if u0 == 0 else "act", "tensor_copy" if u0 == 0 else "copy",
                    reads=[pb.whole()], writes=[VT[vc, u0:u0 + 8]],
                    out=VT[vc, u0:u0 + 8].ap, in_=pb.whole().ap.rearrange("p (a b) -> p a b", a=8))
        if astop == "v":
            return
        for kc in range(6):
            project(6 + kc, Tn(self.sb, K.off + kc * T * 2, BF16, (T,)), 1.0)
        if astop == "k":
            return

        NB = 84
        bk0 = {}
        idx = 0
        for kc in range(6):
            bk0[kc] = idx
            idx += DIL[kc // 2]
        assert idx == 42
        if mode == "x":
            bnc_t = nc.dram_tensor("bnc_at%d" % li, [P, NB, P], BF16)
            bnc = bnc_t
        elif mode == "prod":
            bnc = self.halo_out
        if mode in ("x", "prod"):
            bsem = self.new_sem("bat%d" % li)
            sends = []
            for kc in range(6):
                d = DIL[kc // 2]
                per = 16 // d
                kv = Tn(self.sb, K.off + kc * T * 2, BF16, (d, per, P))
                src_k = kv[:, per - 1, :]
                src_v_ap = VT.ap[:, kc, per - 1::per, :]
                src_v = Ref(src_v_ap, "sb", VT[kc].gr)
                for (src, b0) in ((src_k, bk0[kc]), (src_v, 42 + bk0[kc])):
                    sends.append(S.I("sp", "dma_start", reads=[src], writes=[dref(None, "bnc_at%d" % li, [b0])],
                                     dma_sem=bsem, out=bnc[:, b0:b0 + d, :], in_=src.ap))
        if mode == "prod":
            self.tail_ops += sends
            return
        if mode == "x":
            NCH = int(os.environ.get("ATT_NCH", "7"))
            CB = NB // NCH
            for ch in range(NCH):
                bch = nc.dram_tensor("bch_at%d_%d" % (li, ch), [P, CB * P], BF16)
                gch = nc.dram_tensor("gch_at%d_%d" % (li, ch), [NCORES * P, CB * P], BF16)
                nm = "at%d_%d" % (li, ch)
                S.I("pool", "dma_start", reads=[dref(None, "bnc_at%d" % li, list(range(84)))],
                    writes=[dref(None, "bch_" + nm, [0])], dma_sem=self.new_sem("c1" + nm),
                    out=bch[:, :].rearrange("p (a b) -> p a b", a=CB), in_=bnc_t[:, ch * CB:(ch + 1) * CB, :])
                S.I("pool", "collective_compute", reads=[dref(None, "bch_" + nm, [0])],
                    writes=[dref(None, "gch_" + nm, [0])], dma_sem=self.new_sem("c2" + nm), inc=1,
                    kind="AllGather", op=ALU.bypass, replica_groups=[list(range(NCORES))],
                    ins=[bch.ap().opt()], outs=[gch.ap().opt()])

                def ld(e, gch=gch, ch=ch, CB=CB):
                    prev = self.prev_rank()
                    return e.dma_start(out=HAL[ch * CB:(ch + 1) * CB].ap,
                                       in_=gch[bass.ds(prev * P, P), :].rearrange("p (a b) -> p a b", a=CB))
                S.op("pool", ld, reads=[dref(None, "gch_" + nm, [0])], writes=[HAL[ch * CB:(ch + 1) * CB]],
                     dma_sem=self.new_sem("c3" + nm))
        else:
            S.I("sp", "dma_start", writes=[HAL.whole()], dma_sem=self.new_sem("hat%d" % li),
                out=HAL.whole().ap, in_=self.halo_in)

        if astop == "x":
            return
        for qc in range(6):
            project(qc, Tn(self.sb, Q.off + qc * T * 2, BF16, (T,)), 0.125)
        if astop == "q":
            return

        pti = 0
        sti = 0
        kcs = [int(c) for c in os.environ.get("DBG_KCS", "012345")]
        for kc in kcs:
            g = kc // 2
            d = DIL[g]
            per = 16 // d
            cls = kc % 2
            Krow = Tn(self.sb, K.off + kc * T * 2, BF16, (16, P))
            Qrow = Tn(self.sb, Q.off + kc * T * 2, BF16, (16, P))
            o_row = O.ap[:, kc, :]
            den_row = DEN.ap[:, cls, :]
            for up in range(8):
                od = self.psbank(self.next_bank())
                dd = self.psbank(self.next_bank())
                units = (2 * up, 2 * up + 1)
                for hp in range(2):
                    h = 2 * kc + hp
                    ps_ = slice(hp * 64, (hp + 1) * 64)
                    sb_ = self.psbank(self.next_bank())
                    halo_unit = []
                    for ui, u in enumerate(units):
                        nb = u % per
                        r = u // per
                        if nb == 0:
                            kprev = HAL.pslice(hp * 64, (hp + 1) * 64, (bk0[kc] + r,))
                        else:
                            kprev = Krow.pslice(hp * 64, (hp + 1) * 64, (u - 1,))
                        halo_unit.append(nb == 0)
                        kcur = Krow.pslice(hp * 64, (hp + 1) * 64, (u,))
                        qq = Qrow.pslice(hp * 64, (hp + 1) * 64, (u,))
                        for bi, kk in enumerate((kprev, kcur)):
                            c0 = ui * 256 + bi * 128
                            S.I("pe", "matmul", reads=[kk, qq], writes=[sb_[c0:c0 + P]],
                                out=sb_[c0:c0 + P].ap, lhsT=kk.ap, rhs=qq.ap, start=True, stop=True)
                    st = STMP[sti % 2]
                    sti += 1
                    G = BIAS[h]
                    for ui, u in enumerate(units):
                        c0 = ui * 256
                        if halo_unit[ui]:
                            S.I("dve", "scalar_tensor_tensor", reads=[sb_[c0:c0 + P], hb, BIAS[h, 0:P]],
                                writes=[st[c0:c0 + P]],
                                out=st[c0:c0 + P].ap, in0=sb_[c0:c0 + P].ap, scalar=hb.ap, in1=BIAS[h, 0:P].ap,
                                op0=ALU.add, op1=ALU.add)
                            S.I("dve", "tensor_tensor", reads=[sb_[c0 + P:c0 + 256], BIAS[h, P:256]],
                                writes=[st[c0 + P:c0 + 256]],
                                out=st[c0 + P:c0 + 256].ap, in0=sb_[c0 + P:c0 + 256].ap, in1=BIAS[h, P:256].ap,
                                op=ALU.add)
                        else:
                            S.I("dve", "tensor_tensor", reads=[sb_[c0:c0 + 256], G], writes=[st[c0:c0 + 256]],
                                out=st[c0:c0 + 256].ap, in0=sb_[c0:c0 + 256].ap, in1=G.ap, op=ALU.add)
                    pt = PT[pti % 3]
                    pti += 1
                    lstop = os.environ.get("DBG_LOOP", "")
                    if lstop == "s":
                        continue
                    S.I("act", "activation", reads=[st.whole()], writes=[pt.whole()],
                        out=pt.whole().ap, in_=st.whole().ap, func=AF.Exp)
                    if lstop == "e":
                        continue
                    for ui, u in enumerate(units):
                        nb = u % per
                        r = u // per
                        if nb == 0:
                            vprev = HAL[42 + bk0[kc] + r, hp * 64:(hp + 1) * 64]
                        else:
                            vprev = VT[kc, u - 1, hp * 64:(hp + 1) * 64]
                        vcur = VT[kc, u, hp * 64:(hp + 1) * 64]
                        oc0 = ui * P
                        o_out = od.pslice(hp * 64, (hp + 1) * 64, (slice(oc0, oc0 + P),))
                        d_out = dd.pslice(hp * 64, (hp + 1) * 64, (slice(oc0, oc0 + P),))
                        for bi, vv in enumerate((vprev, vcur)):
                            pblk = pt[ui * 256 + bi * P: ui * 256 + (bi + 1) * P]
                            S.I("pe", "matmul", reads=[vv, pblk], writes=[o_out],
                                out=o_out.ap, lhsT=vv.ap, rhs=pblk.ap, start=(bi == 0), stop=(bi == 1))
                        for bi in range(2):
                            pblk = pt[ui * 256 + bi * P: ui * 256 + (bi + 1) * P]
                            S.I("pe", "matmul", reads=[self.ONES[0:64], pblk], writes=[d_out],
                                out=d_out.ap, lhsT=self.ONES[0:64].ap, rhs=pblk.ap, start=(bi == 0), stop=(bi == 1))
                if os.environ.get("DBG_LOOP", "") in ("s", "e", "o"):
                    continue
                od3 = od[0:2 * P].ap.rearrange("p (u c) -> p u c", u=2)
                dd3 = dd[0:2 * P].ap.rearrange("p (u c) -> p u c", u=2)
                u = units[0]
                nb = u % per
                r = u // per
                if d == 1:
                    o_dst = o_row.rearrange("p (a b) -> p a b", b=P)[:, u:u + 2, :]
                    d_dst = den_row.rearrange("p (a b) -> p a b", b=P)[:, u:u + 2, :]
                elif d == 4:
                    o_dst = o_row.rearrange("p (a b r) -> p r a b", r=4, b=P)[:, r, nb:nb + 2, :]
                    d_dst = den_row.rearrange("p (a b r) -> p r a b", r=4, b=P)[:, r, nb:nb + 2, :]
                else:
                    o_dst = o_row.rearrange("p (b r) -> p r b", r=16)[:, r:r + 2, :]
                    d_dst = den_row.rearrange("p (b r) -> p r b", r=16)[:, r:r + 2, :]
                o_ref = Ref(o_dst, "sb", O[kc].gr)
                d_ref = Ref(d_dst, "sb", DEN[cls].gr)
                S.I("act", "activation", reads=[od.whole()], writes=[o_ref], nowaw=True,
                    out=o_dst, in_=od3, func=AF.Copy)
                if g == 0:
                    S.I("dve", "tensor_copy", reads=[dd.whole()], writes=[d_ref], nowaw=True,
                        out=d_dst, in_=dd3)
                else:
                    S.I("dve", "tensor_tensor", reads=[dd.whole(), d_ref], writes=[d_ref],
                        out=d_dst, in0=dd3, in1=d_dst, op=ALU.add)

        if astop == "a":
            return
        for cls in range(2):
            for tt in range(NT):
                ts = slice(tt * TT, (tt + 1) * TT)
                S.I("dve", "reciprocal", reads=[DEN[cls, ts]], writes=[DEN[cls, ts]],
                    out=DEN[cls, ts].ap, in_=DEN[cls, ts].ap)
        for tt in range(NT):
            ts = slice(tt * TT, (tt + 1) * TT)
            for kc in range(6):
                S.I("dve", "tensor_tensor", reads=[O[kc, ts], DEN[kc % 2, ts]], writes=[O[kc, ts]],
                    out=O[kc, ts].ap, in0=O[kc, ts].ap, in1=DEN[kc % 2, ts].ap, op=ALU.mult)
        for k in range(KC):
            S.I("sp", "dma_start", reads=[dref(None, "xpark%d" % li, [k])], writes=[self.X[k]], dma_sem=park_sems[k],
                out=self.X[k].ap, in_=xpark[:, k, :])
        wo = [self.load_piece(wbase(18 + ci)) for ci in range(6)]
        for tt in range(NT):
            ts = slice(tt * TT, (tt + 1) * TT)
            for dc in range(KC):
                pb = self.psbank(self.next_bank())
                for ci in range(6):
                    wsl = wo[ci][dc * P:(dc + 1) * P]
                    S.I("pe", "matmul", reads=[wsl, O[ci, ts]], writes=[pb.whole()],
                        out=pb.whole().ap, lhsT=wsl.ap, rhs=O[ci, ts].ap, start=(ci == 0), stop=(ci == 5))
                S.I("dve", "tensor_tensor", reads=[pb.whole(), self.X[dc, ts]], writes=[self.X[dc, ts]],
                    out=self.X[dc, ts].ap, in0=pb.whole().ap, in1=self.X[dc, ts].ap, op=ALU.add)


RUN_KW = {}
LAST_RES = None
ALL_STAGES = [("conv", 0), ("ffn", 0), ("attn", 1), ("ffn", 1), ("conv", 2), ("ffn", 2), ("attn", 3), ("ffn", 3)]


def t5_bucket_np(dist):
    n_buckets, max_distance = 32, 2048
    max_exact = n_buckets // 2
    n = np.maximum(dist, 0)
    nf = np.maximum(n, 1).astype(np.float32)
    large = max_exact + (np.log(nf / np.float32(max_exact)) / np.float32(np.log(max_distance / max_exact))
                         * np.float32(n_buckets - max_exact)).astype(np.int32)
    large = np.minimum(large, n_buckets - 1)
    return np.where(n < max_exact, n, large)


def bias_tables(rel_bias):
    out = np.full((P, NH, 256), NEG, np.float32)
    jj = np.arange(P)[:, None]
    ii = np.arange(P)[None, :]
    for h in range(NH):
        d = (1, 4, 16)[h // 4]
        dist_prev = ii + P - jj
        dist_cur = ii - jj
        bp = rel_bias[t5_bucket_np(dist_prev * d), h]
        bc = rel_bias[t5_bucket_np(dist_cur * d), h]
        out[:, h, 0:P] = np.where(jj >= ii, bp, NEG)
        out[:, h, P:256] = np.where(jj <= ii, bc, NEG)
    return np.ascontiguousarray(out.reshape(P, NH * 256))


def prep_common(inputs):
    c = {}
    g = lambda k: np.asarray(inputs[k], np.float32)
    cst = np.zeros((P, CL.n), np.float32)

    def put(name, arr):
        a, b = CL.sl(name)
        assert arr.shape == (P, b - a), (name, arr.shape, b - a)
        cst[:, a:b] = arr

    put("norm_mix", colmajor(g("norm_mix"), KC))
    put("norm_ffn", colmajor(g("norm_ffn"), KC))
    put("final_norm", colmajor(g("final_norm"), KC))
    put("b_pw1", colmajor(g("conv_b_pw1"), 16))
    wdw = g("conv_w_dw").reshape(2, CW, KC, P).transpose(3, 0, 2, 1).reshape(P, -1)
    put("w_dw", np.ascontiguousarray(wdw))
    put("b_dw", colmajor(g("conv_b_dw"), KC))
    put("ln_g", colmajor(g("conv_ln_g"), KC))
    put("ln_b", colmajor(g("conv_ln_b"), KC))
    put("b_pw2", colmajor(g("conv_b_pw2"), KC))
    c["cst"] = cst
    wg = g("ffn_w_gate").reshape(DEPTH, KC, P, FC, P).transpose(0, 3, 2, 1, 4).reshape(DEPTH, FC, P, 1024)
    wu = g("ffn_w_up").reshape(DEPTH, KC, P, FC, P).transpose(0, 3, 2, 1, 4).reshape(DEPTH, FC, P, 1024)
    wd = g("ffn_w_down").reshape(DEPTH, FC, P, D)
    c["wffn"] = np.ascontiguousarray(np.stack([wg, wu, wd], axis=1))
    w1 = g("conv_w_pw1").reshape(2, KC, P, 16, P).transpose(0, 3, 2, 1, 4).reshape(2, 16, P, 1024)
    w2 = g("conv_w_pw2").reshape(2, KC, P, D)
    c["wconv"] = np.ascontiguousarray(np.concatenate([w1, w2], axis=1))
    c["identd"] = np.eye(P, dtype=np.float32)
    wq = g("attn_w_qkv").reshape(2, KC, P, 18, P).transpose(0, 3, 2, 1, 4).reshape(2, 18, P, 1024)
    wo = g("attn_w_o").reshape(2, 6, P, D)
    c["wattn"] = np.ascontiguousarray(np.concatenate([wq, wo], axis=1))
    c["bmask"] = bias_tables(g("rel_bias"))
    return c


def x_to_cores(x):
    xf = np.asarray(x, np.float32).reshape(NCORES, T, KC, P)
    return [np.ascontiguousarray(xf[c].transpose(2, 1, 0)) for c in range(NCORES)]


def cores_to_x(ys, B, S):
    out = np.stack([y.transpose(2, 1, 0).reshape(T, D) for y in ys], axis=0)
    return np.ascontiguousarray(out.reshape(B, S, D)).astype(np.float32)


def launch(common, xs, stages, final, halo_in=None):
    b = Builder(stages, final)
    nc = b.build()
    kinds = [k for (k, _) in stages]
    in_maps = []
    for c in range(NCORES):
        m = {"xT": xs[c]}
        cst = common["cst"].copy()
        a, _ = CL.sl("hv")
        cst[:, a] = 0.0 if c % 4 == 0 else 1.0
        a, _ = CL.sl("hb")
        cst[:, a] = NEG if c % 4 == 0 else 0.0
        m["cst"] = cst
        for k in ("wffn", "wconv", "identd", "wattn", "bmask"):
            m[k] = common[k]
        if "attn_in" in kinds:
            m["halo_in"] = halo_in[(c - 1) % NCORES]
        if "conv_in" in kinds:
            m["xh_in"] = np.ascontiguousarray(xs[(c - 1) % NCORES][:, :, T - HALO:])
        in_maps.append(m)
    res = run_bass_kernel_spmd(nc, in_maps, core_ids=list(range(NCORES)), **RUN_KW)
    global LAST_RES
    LAST_RES = res
    ys = [r["yT"] for r in res.results]
    ho = [r["halo_out"] for r in res.results] if "attn_prod" in kinds else None
    return ys, ho


def run_stages(inputs, x, stages, final):
    common = prep_common(inputs)
    ys, _ = launch(common, x_to_cores(x), stages, final)
    B_, S_ = np.asarray(inputs["x"]).shape[:2]
    return cores_to_x(ys, B_, S_)


def kernel_unfused(inputs):
    common = prep_common(inputs)
    xs = x_to_cores(inputs["x"])
    xs, h1 = launch(common, xs, [("conv_in", 0), ("ffn", 0), ("attn_prod", 1)], False)
    xs, _ = launch(common, xs, [("attn_in", 1), ("ffn", 1)], False, h1)
    xs, h3 = launch(common, xs, [("conv_in", 2), ("ffn", 2), ("attn_prod", 3)], False)
    ys, _ = launch(common, xs, [("attn_in", 3), ("ffn", 3)], True, h3)
    B_, S_ = np.asarray(inputs["x"]).shape[:2]
    return cores_to_x(ys, B_, S_)


FUSED = False


def kernel(**inputs):
    if FUSED:
        return run_stages(inputs, inputs["x"], ALL_STAGES, True)
    return kernel_unfused(inputs)
```

```python
import numpy as np
import ml_dtypes
import concourse.bass as bass
import concourse.mybir as mybir
from concourse.bass_utils import run_bass_kernel_spmd

F32 = mybir.dt.float32
BF16 = mybir.dt.bfloat16
AF = mybir.ActivationFunctionType
ALU = mybir.AluOpType

NCORES = 8
P = 128
T = 2048
TT = 512
NT = T // TT
D = 1024
KC = D // P
DFF = 2816
FC = DFF // P
DEPTH = 4
EPS = 1e-6
NEG = -1e30
CW = 31
HALO = 32
DATT = 768
NH = 12

GRAN = 256


class Sem:
    def __init__(self, handle):
        self.h = handle
        self.count = 0


class Op:
    __slots__ = ("eng", "fn", "deps", "needed", "idx", "sem", "val", "inc", "pos")

    def __init__(self, eng, fn):
        self.eng = eng
        self.fn = fn
        self.deps = []
        self.needed = False
        self.idx = None
        self.sem = None
        self.val = None
        self.inc = 16
        self.pos = None


class Ref:
    __slots__ = ("ap", "space", "gr")

    def __init__(self, ap, space, gr):
        self.ap = ap
        self.space = space
        self.gr = gr


class Sched:
    ENGS = ("pe", "act", "dve", "pool", "sp")

    def __init__(self, nc):
        self.nc = nc
        self.q = {e: [] for e in self.ENGS}
        self.mem = {}
        self.nops = 0

    def _key(self, op):
        return op.sem if op.sem is not None else op.eng

    def op(self, eng, fn, reads=(), writes=(), dma_sem=None, inc=16, nowaw=False, extra=()):
        o = Op(eng, fn)
        o.pos = self.nops
        self.nops += 1
        if dma_sem is not None:
            o.sem = dma_sem
            o.inc = inc
            dma_sem.count += inc
            o.val = dma_sem.count
        deps = {}
        mykey = self._key(o)

        def add(d):
            k = self._key(d)
            if d.sem is None and d.eng == eng and o.sem is None:
                pass
            cur = deps.get(k)
            if cur is None or d.pos > cur.pos:
                deps[k] = d

        for r in reads:
            for g in r.gr:
                st = self.mem.get((r.space, g))
                if st is None:
                    st = self.mem[(r.space, g)] = [{}, {}]
                for d in st[0].values():
                    add(d)
                st[1][mykey] = o
        for w in writes:
            for g in w.gr:
                st = self.mem.get((w.space, g))
                if st is None:
                    st = self.mem[(w.space, g)] = [{}, {}]
                for d in st[1].values():
                    if d is o:
                        continue
                    if d.sem is None and o.sem is None and d.eng == eng and eng == "pe":
                        continue
                    add(d)
                for d in st[0].values():
                    if d is o:
                        continue
                    if d.sem is None and o.sem is None and d.eng == eng and eng == "pe":
                        continue
                    if nowaw:
                        continue
                    add(d)
                if not nowaw:
                    st[0] = {mykey: o}
                    st[1] = {}
                else:
                    st[0][mykey] = o
        for d in extra:
            add(d)
        for d in deps.values():
            if d is o:
                continue
            d.needed = True
            snap = d.sem.count if d.sem is not None else None
            if d.sem is not None and d.sem is o.sem:
                snap = d.val
            o.deps.append((d, snap))
        self.q[eng].append(o)
        return o

    def I(self, eng, method, reads=(), writes=(), dma_sem=None, inc=16, nowaw=False, extra=(), **kw):
        return self.op(eng, lambda e: getattr(e, method)(**kw), reads=reads, writes=writes,
                       dma_sem=dma_sem, inc=inc, nowaw=nowaw, extra=extra)

    def emit(self, engines, sems):
        for e in self.ENGS:
            n = 0
            for o in self.q[e]:
                if o.sem is None and o.needed:
                    n += 1
                    o.idx = n
        for e in self.ENGS:
            self._emit_eng(e, engines[e], sems)

    def _emit_eng(self, e, eng, sems):
        seen = {}
        for o in self.q[e]:
            waits = {}
            for (d, snap) in o.deps:
                if d.sem is not None:
                    k, h, v = id(d.sem), d.sem.h, snap
                else:
                    k, h, v = d.eng, sems[d.eng].h, d.idx
                if seen.get(k, 0) >= v:
                    continue
                if k not in waits or waits[k][1] < v:
                    waits[k] = (h, v)
            for k, (h, v) in waits.items():
                eng.wait_ge(h, v)
                seen[k] = v
            if o.fn is None:
                continue
            ins = o.fn(eng)
            if o.sem is not None:
                ins.then_inc(o.sem.h, o.inc)
            elif o.needed:
                ins.then_inc(sems[e].h, 1)


class Mem:
    def __init__(self, space, base_ap_by_dtype, gran=GRAN):
        self.space = space
        self.gran = gran
        self.base = base_ap_by_dtype
        self.off = 0

    def alloc(self, nbytes, align=GRAN):
        self.off = (self.off + align - 1) // align * align
        o = self.off
        self.off += nbytes
        return o


class Tn:
    def __init__(self, mem, off, dtype, shape):
        self.mem = mem
        self.off = off
        self.dtype = dtype
        self.esz = 4 if dtype == F32 else 2
        self.shape = tuple(shape)
        n = int(np.prod(shape))
        self.nbytes = n * self.esz
        e0 = off // self.esz
        ap = mem.base[dtype][:, e0:e0 + n]
        if len(shape) > 1:
            names = " ".join("d%d" % i for i in range(len(shape)))
            kw = {"d%d" % i: shape[i] for i in range(1, len(shape))}
            ap = ap.rearrange("p (%s) -> p %s" % (names, names), **kw)
        self.ap = ap
        st = [self.esz]
        for s in reversed(self.shape[1:]):
            st.insert(0, st[0] * s)
        self.strides = st

    def __getitem__(self, idx):
        if not isinstance(idx, tuple):
            idx = (idx,)
        idx = idx + (slice(None),) * (len(self.shape) - len(idx))
        ap = self.ap[(slice(None),) + idx]
        return Ref(ap, self.mem.space, self.granules(idx))

    def granules(self, idx):
        rng = []
        for i, s in zip(idx, self.shape):
            if isinstance(i, int):
                rng.append((i, 1, 1))
            else:
                a, b, c = i.indices(s)
                rng.append((a, (b - a + c - 1) // c, c))
        out = set()
        inner = rng[-1]
        outer = rng[:-1]

        def rec(d, base):
            if d == len(outer):
                a, n, c = inner
                lo = base + a * self.strides[-1]
                hi = base + (a + (n - 1) * c + 1) * self.strides[-1]
                gsz = self.mem.gran
                for g in range(lo // gsz, (hi - 1) // gsz + 1):
                    out.add(g)
                return
            a, n, c = outer[d]
            for j in range(n):
                rec(d + 1, base + (a + j * c) * self.strides[d])

        rec(0, self.off)
        return out

    def whole(self):
        return self[tuple(slice(None) for _ in self.shape)]

    def pslice(self, p0, p1, idx):
        r = self[idx]
        return Ref(r.ap[p0:p1], r.space, r.gr)


def dref(ap, name, ids):
    return Ref(ap, "dram:" + name, set(ids))


class CstLayout:
    def __init__(self):
        self.cols = {}
        self.n = 0

    def add(self, name, ncols):
        self.cols[name] = (self.n, ncols)
        self.n += ncols

    def sl(self, name):
        a, n = self.cols[name]
        return a, a + n


def cst_layout():
    L = CstLayout()
    L.add("norm_mix", DEPTH * KC)
    L.add("norm_ffn", DEPTH * KC)
    L.add("final_norm", KC)
    L.add("b_pw1", 2 * 16)
    L.add("w_dw", 2 * KC * CW)
    L.add("b_dw", 2 * KC)
    L.add("ln_g", 2 * KC)
    L.add("ln_b", 2 * KC)
    L.add("b_pw2", 2 * KC)
    L.add("hv", 1)
    L.add("hb", 1)
    L.n = (L.n + 63) // 64 * 64
    return L


CL = cst_layout()


def colmajor(v, nch):
    v = np.asarray(v, np.float32)
    lead = v.shape[:-1]
    v = v.reshape(lead + (nch, P))
    v = np.moveaxis(v, -1, 0)
    return np.ascontiguousarray(v).reshape(P, -1)


class Builder:
    def __init__(self, stages, final=True):
        self.stages = stages
        self.final = final
        nc = bass.Bass("TRN2", target_bir_lowering=False)
        self.nc = nc
        self.S = Sched(nc)

    def declare(self):
        nc = self.nc
        self.xT = nc.dram_tensor("xT", [P, KC, T], F32, kind="ExternalInput").ap()
        self.yT = nc.dram_tensor("yT", [P, KC, T], F32, kind="ExternalOutput").ap()
        self.cst = nc.dram_tensor("cst", [P, CL.n], F32, kind="ExternalInput").ap()
        self.wffn = nc.dram_tensor("wffn", [DEPTH, 3, FC, P, 1024], F32, kind="ExternalInput").ap()
        self.wconv = nc.dram_tensor("wconv", [2, 24, P, 1024], F32, kind="ExternalInput").ap()
        self.identd = nc.dram_tensor("identd", [P, P], F32, kind="ExternalInput").ap()
        self.wattn = nc.dram_tensor("wattn", [2, 24, P, 1024], F32, kind="ExternalInput").ap()
        self.bmask = nc.dram_tensor("bmask", [P, NH * 256], F32, kind="ExternalInput").ap()
        kinds = [k for (k, _) in self.stages]
        if "conv_in" in kinds:
            self.xh_in = nc.dram_tensor("xh_in", [P, KC, HALO], F32, kind="ExternalInput").ap()
        if "attn_in" in kinds:
            self.halo_in = nc.dram_tensor("halo_in", [P, 84, P], BF16, kind="ExternalInput").ap()
        if "attn_prod" in kinds:
            self.halo_out = nc.dram_tensor("halo_out", [P, 84, P], BF16, kind="ExternalOutput").ap()

    def build(self):
        nc = self.nc
        self.declare()
        SB_BYTES = 207 * 1024
        with (
            nc.sbuf_tensor("sb16", [P, SB_BYTES // 2], BF16) as sb16,
            nc.psum_tensor("ps32", [P, 4096], F32) as ps32,
            nc.semaphore("s_pe") as s_pe, nc.semaphore("s_act") as s_act,
            nc.semaphore("s_dve") as s_dve, nc.semaphore("s_pool") as s_pool,
            nc.semaphore("s_sp") as s_sp,
        ):
            self.sems = {"pe": Sem(s_pe), "act": Sem(s_act), "dve": Sem(s_dve),
                         "pool": Sem(s_pool), "sp": Sem(s_sp)}
            sb32 = sb16[:, :].bitcast(F32)
            ps16 = ps32[:, :].bitcast(BF16)
            self.sb = Mem("sb", {BF16: sb16[:, :], F32: sb32})
            self.ps = Mem("ps", {F32: ps32[:, :], BF16: ps16}, gran=2048)
            self.SB_BYTES = SB_BYTES
            self._sem_cms = []
            try:
                self.program()
                with nc.Block() as block:
                    engines = {}

                    def run(name):
                        def f(eng):
                            self.S._emit_eng(name, eng, self.sems)
                        return f
                    for e in Sched.ENGS:
                        n = 0
                        for o in self.S.q[e]:
                            if o.sem is None and o.needed:
                                n += 1
                                o.idx = n
                    block.tensor(run("pe"))
                    block.scalar(run("act"))
                    block.vector(run("dve"))
                    block.gpsimd(run("pool"))
                    block.sync(run("sp"))
            finally:
                for cm in reversed(self._sem_cms):
                    cm.__exit__(None, None, None)
        return nc

    def prev_rank(self):
        if getattr(self, "_prev", None) is None:
            pid = self.nc.partition_id()
            self._prev = self.nc.gpsimd.snap((pid + (NCORES - 1)) % NCORES)
        return self._prev

    def new_sem(self, name):
        cm = self.nc.semaphore(name)
        h = cm.__enter__()
        self._sem_cms.append(cm)
        return Sem(h)

    def sbt(self, off, dtype, shape):
        return Tn(self.sb, off, dtype, shape)

    def psbank(self, b, dtype=F32, shape=None):
        if shape is None:
            shape = (512,) if dtype == F32 else (1024,)
        return Tn(self.ps, b * 2048, dtype, shape)

    def program(self):
        S = self.S
        sb = self.sb
        o_cst = sb.alloc(CL.n * 4)
        self.CST = self.sbt(o_cst, F32, (CL.n,))
        o_id = sb.alloc(256)
        self.ONES = self.sbt(o_id, BF16, (128,))
        o_id2 = sb.alloc(256)
        self.IDENT = self.sbt(o_id2, BF16, (128,))
        o_x = sb.alloc(KC * T * 4)
        self.X = self.sbt(o_x, F32, (KC, T))
        self.NSLOT = 16
        o_ring = sb.alloc(self.NSLOT * 2048)
        self.RING = [self.sbt(o_ring + i * 2048, BF16, (1024,)) for i in range(self.NSLOT)]
        self.ring_sems = [self.new_sem("ring%d" % i) for i in range(self.NSLOT)]
        self.ring_next = 0
        self.scr0 = sb.alloc(0)
        self.scr_bytes = self.SB_BYTES - self.scr0
        self.psn = 0
        self.held = set()
        self.io_sem = self.new_sem("io")
        self.tail_ops = []
        self.out_sem = self.new_sem("outs")

        S.op("sp", lambda e: e.dma_start(out=self.CST.whole().ap, in_=self.cst[:, :]),
             writes=[self.CST.whole()], dma_sem=self.io_sem)
        S.op("pool", lambda e: e.memset(self.ONES.whole().ap, 1.0), writes=[self.ONES.whole()])
        S.op("pool", lambda e: e.dma_start(out=self.IDENT.whole().ap, in_=self.identd[:, :]),
             writes=[self.IDENT.whole()], dma_sem=self.new_sem("identsem"))
        for k in range(KC):
            S.op("sp", (lambda k: lambda e: e.dma_start(out=self.X[k].ap, in_=self.xT[:, k, :]))(k),
                 writes=[self.X[k]], dma_sem=self.new_sem("xin%d" % k))

        for st in self.stages:
            kind, li = st
            if kind == "ffn":
                self.ffn(li)
            elif kind == "conv":
                self.conv(li)
            elif kind == "conv_in":
                self.conv(li, "in")
            elif kind == "attn":
                self.attn(li, "x")
            elif kind == "attn_in":
                self.attn(li, "in")
            elif kind == "attn_prod":
                self.attn(li, "prod")
        self.final_out()

    def cst_col(self, name, j):
        a, _ = CL.sl(name)
        return self.CST[a + j:a + j + 1]

    def next_bank(self):
        while True:
            b = self.psn % 8
            self.psn += 1
            if b not in self.held:
                return b

    def load_piece(self, src_ap):
        s = self.ring_next % self.NSLOT
        self.ring_next += 1
        slot = self.RING[s]
        self.S.op("pool", lambda e: e.dma_start(out=slot.whole().ap, in_=src_ap),
                  writes=[slot.whole()], dma_sem=self.ring_sems[s])
        return slot

    def rmsnorm_tile(self, tt, gname, gidx, H, SQ, RSA, RS, out_f32=None):
        ts = slice(tt * TT, (tt + 1) * TT)
        src = lambda k: self.X[k, ts]
        if out_f32 is None:
            dst = lambda k: H[k, ts]
        else:
            dst = lambda k: out_f32[k]
        self.rmsnorm_gen(src, self.X[:, ts], TT, gname, gidx, dst, SQ, RSA, RS)

    def rmsnorm_gen(self, src, src_all, n, gname, gidx, dst, SQ, RSA, RS):
        S = self.S
        sq_all = SQ[:, 0:n]
        S.op("act", lambda e: e.activation(out=sq_all.ap, in_=src_all.ap, func=AF.Square),
             reads=[src_all], writes=[sq_all])
        b = self.next_bank()
        pb = self.psbank(b)[0:n]
        for k in range(KC):
            S.op("pe", (lambda k: lambda e: e.matmul(pb.ap, lhsT=self.ONES.whole().ap, rhs=SQ[k, 0:n].ap,
                                                     start=(k == 0), stop=(k == KC - 1)))(k),
                 reads=[self.ONES.whole(), SQ[k, 0:n]], writes=[pb])
        rsa = RSA[0:n]
        rs = RS[0:n]
        S.op("act", lambda e: e.activation(out=rsa.ap, in_=pb.ap, func=AF.Sqrt,
                                           bias=self.EPSC.whole().ap, scale=1.0 / D),
             reads=[pb, self.EPSC.whole()], writes=[rsa])
        S.op("dve", lambda e: e.reciprocal(out=rs.ap, in_=rsa.ap), reads=[rsa], writes=[rs])
        for k in range(KC):
            g = self.cst_col(gname, gidx * KC + k)
            S.op("dve", (lambda k, g: lambda e: e.scalar_tensor_tensor(
                out=dst(k).ap, in0=src(k).ap, scalar=g.ap, in1=rs.ap,
                op0=ALU.mult, op1=ALU.mult))(k, g),
                reads=[src(k), g, rs], writes=[dst(k)])

    def scratch(self):
        return [self.scr0]

    def salloc(self, cur, nbytes):
        o = (cur[0] + GRAN - 1) // GRAN * GRAN
        cur[0] = o + nbytes
        assert cur[0] <= self.SB_BYTES, ("scratch overflow", cur[0], self.SB_BYTES)
        return o

    def ensure_eps(self):
        if hasattr(self, "EPSC"):
            return
        o = self.SB_BYTES - GRAN
        self.SB_BYTES -= GRAN
        self.EPSC = self.sbt(o, F32, (1,))
        self.S.op("pool", lambda e: e.memset(self.EPSC.whole().ap, EPS), writes=[self.EPSC.whole()])

    def ffn(self, li):
        S = self.S
        self.ensure_eps()
        cur = self.scratch()
        H = self.sbt(self.salloc(cur, KC * T * 2), BF16, (KC, T))
        NH_F = FC // 2
        A = self.sbt(self.salloc(cur, NH_F * T * 2), BF16, (NH_F, T))
        SQ = self.sbt(self.salloc(cur, KC * TT * 2), BF16, (KC, TT))
        RSA = [self.sbt(self.salloc(cur, TT * 4), F32, (TT,)) for _ in range(2)]
        RS = [self.sbt(self.salloc(cur, TT * 4), F32, (TT,)) for _ in range(2)]
        SG = [self.sbt(self.salloc(cur, TT * 4), F32, (TT,)) for _ in range(3)]
        for tt in range(NT):
            self.rmsnorm_tile(tt, "norm_ffn", li, H, SQ, RSA[tt % 2], RS[tt % 2])
        sgi = 0
        for hf in range(2):
            fs = list(range(hf * NH_F, (hf + 1) * NH_F))
            wd_slots = {}
            for fi, f in enumerate(fs):
                wg = self.load_piece(self.wffn[li, 0, f])
                wu = self.load_piece(self.wffn[li, 1, f])
                wg3 = Tn(self.sb, wg.off, BF16, (KC, P))
                wu3 = Tn(self.sb, wu.off, BF16, (KC, P))
                for tt in range(NT):
                    ts = slice(tt * TT, (tt + 1) * TT)
                    bg = self.psbank(self.next_bank())
                    bu = self.psbank(self.next_bank())
                    for (w3, pb) in ((wg3, bg), (wu3, bu)):
                        for k in range(KC):
                            S.op("pe", (lambda w3, pb, k, ts: lambda e: e.matmul(
                                pb.whole().ap, lhsT=w3[k].ap, rhs=H[k, ts].ap,
                                start=(k == 0), stop=(k == KC - 1)))(w3, pb, k, ts),
                                reads=[w3[k], H[k, ts]], writes=[pb.whole()])
                    sg = SG[sgi % 3]
                    sgi += 1
                    S.op("act", (lambda sg, bg: lambda e: e.activation(out=sg.whole().ap, in_=bg.whole().ap,
                                                                       func=AF.Silu))(sg, bg),
                         reads=[bg.whole()], writes=[sg.whole()])
                    S.op("dve", (lambda sg, bu, fi, ts: lambda e: e.tensor_tensor(
                        out=A[fi, ts].ap, in0=bu.whole().ap, in1=sg.whole().ap, op=ALU.mult))(sg, bu, fi, ts),
                        reads=[bu.whole(), sg.whole()], writes=[A[fi, ts]])
            for fi, f in enumerate(fs):
                wd_slots[fi] = self.load_piece(self.wffn[li, 2, f])
            for tt in range(NT):
                ts = slice(tt * TT, (tt + 1) * TT)
                for dc in range(KC):
                    pb = self.psbank(self.next_bank())
                    for fi in range(NH_F):
                        wd = wd_slots[fi]
                        S.op("pe", (lambda wd, pb, fi, ts, dc: lambda e: e.matmul(
                            pb.whole().ap, lhsT=wd[dc * P:(dc + 1) * P].ap, rhs=A[fi, ts].ap,
                            start=(fi == 0), stop=(fi == NH_F - 1)))(wd, pb, fi, ts, dc),
                            reads=[wd[dc * P:(dc + 1) * P], A[fi, ts]], writes=[pb.whole()])
                    S.op("dve", (lambda pb, dc, ts: lambda e: e.tensor_tensor(
                        out=self.X[dc, ts].ap, in0=pb.whole().ap, in1=self.X[dc, ts].ap, op=ALU.add))(pb, dc, ts),
                        reads=[pb.whole(), self.X[dc, ts]], writes=[self.X[dc, ts]])

    def final_out(self):
        S = self.S
        self.ensure_eps()
        cur = self.scratch()
        SQ = self.sbt(self.salloc(cur, KC * TT * 2), BF16, (KC, TT))
        RSA = [self.sbt(self.salloc(cur, TT * 4), F32, (TT,)) for _ in range(2)]
        RS = [self.sbt(self.salloc(cur, TT * 4), F32, (TT,)) for _ in range(2)]
        OUT = [self.sbt(self.salloc(cur, KC * TT * 4), F32, (KC, TT)) for _ in range(2)]
        outs = []
        for tt in range(NT):
            ts = slice(tt * TT, (tt + 1) * TT)
            if self.final:
                self.rmsnorm_tile(tt, "final_norm", 0, None, SQ, RSA[tt % 2], RS[tt % 2], out_f32=OUT[tt % 2])
                src = OUT[tt % 2].whole()
            else:
                src = self.X[:, ts]
            outs.append(S.op("sp", (lambda src, ts: lambda e: e.dma_start(out=self.yT[:, :, ts], in_=src.ap))(src, ts),
                             reads=[src], dma_sem=self.out_sem))
        S.op("sp", None, extra=outs + self.tail_ops)

    def exchange_rows(self, name, src_ref, ncols, dtype, dst_ref, k3=None):
        nc = self.nc
        S = self.S
        bnc = nc.dram_tensor("bnc_" + name, [P, ncols], dtype)
        gat = nc.dram_tensor("gat_" + name, [NCORES * P, ncols], dtype)
        sem1 = self.new_sem("x1_" + name)
        sem2 = self.new_sem("x2_" + name)
        sem3 = self.new_sem("x3_" + name)
        rsh = (lambda a: a) if k3 is None else (lambda a: a.rearrange("p (k t) -> p k t", k=k3))
        S.op("pool", lambda e: e.dma_start(out=rsh(bnc[:, :]), in_=src_ref.ap),
             reads=[src_ref], writes=[dref(None, "bnc_" + name, [0])], dma_sem=sem1)
        S.op("pool", lambda e: e.collective_compute(
            "AllGather", ALU.bypass, replica_groups=[list(range(NCORES))],
            ins=[bnc.ap().opt()], outs=[gat.ap().opt()]),
            reads=[dref(None, "bnc_" + name, [0])], writes=[dref(None, "gat_" + name, [0])],
            dma_sem=sem2, inc=1)

        def ld(e):
            prev = self.prev_rank()
            return e.dma_start(out=dst_ref.ap, in_=rsh(gat[bass.ds(prev * P, P), :]))
        S.op("pool", ld, reads=[dref(None, "gat_" + name, [0])], writes=[dst_ref], dma_sem=sem3)

    def conv(self, li, mode="x"):
        S = self.S
        self.ensure_eps()
        j = li // 2
        TH = T // 2
        cur = self.scratch()
        H = self.sbt(self.salloc(cur, KC * TH * 2), BF16, (KC, TH))
        UW = HALO + TH
        U = self.sbt(self.salloc(cur, KC * UW * 2), BF16, (KC, UW))
        V = self.sbt(self.salloc(cur, KC * TH * 4), F32, (KC, TH))
        DG = self.sbt(self.salloc(cur, CW * P * 2), BF16, (CW, P))
        o_sq = self.salloc(cur, KC * TT * 2)
        SQ = self.sbt(o_sq, BF16, (KC, TT))
        DGS = [DG, self.sbt(o_sq, BF16, (CW, P))]
        RSA = [self.sbt(self.salloc(cur, TT * 4), F32, (TT,)) for _ in range(2)]
        RS = [self.sbt(self.salloc(cur, TT * 4), F32, (TT,)) for _ in range(2)]
        SGT = [self.sbt(self.salloc(cur, TT * 4), F32, (TT,)) for _ in range(2)]
        V16 = [self.sbt(self.salloc(cur, TT * 2), BF16, (TT,)) for _ in range(2)]
        Q16 = [self.sbt(self.salloc(cur, TT * 2), BF16, (TT,)) for _ in range(2)]
        MEAN = self.sbt(self.salloc(cur, TT * 4), F32, (TT,))
        MSQ = self.sbt(self.salloc(cur, TT * 4), F32, (TT,))
        XH = self.sbt(self.salloc(cur, KC * HALO * 4), F32, (KC, HALO))
        HH = self.sbt(self.salloc(cur, KC * HALO * 2), BF16, (KC, HALO))
        wbase = lambda pc: self.wconv[j, pc]
        hv = self.cst_col("hv", 0)

        if mode == "x":
            self.exchange_rows("cv%d" % li, self.X[:, T - HALO:T], KC * HALO, F32, XH.whole(), k3=KC)
        else:
            S.I("sp", "dma_start", writes=[XH.whole()], dma_sem=self.new_sem("xh%d" % li),
                out=XH.whole().ap, in_=self.xh_in)

        cnt = 0
        import os
        stop = os.environ.get("DBG_CONV_STOP", "")
        if stop == "X":
            return
        for hh in range(2):
            t0 = hh * TH
            for t2 in range(2):
                gs = slice(t0 + t2 * TT, t0 + (t2 + 1) * TT)
                ls = slice(t2 * TT, (t2 + 1) * TT)
                self.rmsnorm_gen((lambda gs: lambda k: self.X[k, gs])(gs), self.X[:, gs], TT, "norm_mix", li,
                                 (lambda ls: lambda k: H[k, ls])(ls), SQ, RSA[t2], RS[t2])
            if hh == 0:
                self.rmsnorm_gen(lambda k: XH[k], XH.whole(), HALO, "norm_mix", li,
                                 lambda k: HH[k], SQ, RSA[0], RS[0])
            else:
                S.op("dve", lambda e: e.tensor_copy(out=U[:, 0:HALO].ap, in_=U[:, TH:TH + HALO].ap),
                     reads=[U[:, TH:TH + HALO]], writes=[U[:, 0:HALO]])
            if stop == "A":
                continue
            for co in range(KC):
                wa = self.load_piece(wbase(co))
                wg = self.load_piece(wbase(co + KC))
                wa3 = Tn(self.sb, wa.off, BF16, (KC, P))
                wg3 = Tn(self.sb, wg.off, BF16, (KC, P))
                ba = self.cst_col("b_pw1", j * 16 + co)
                bg = self.cst_col("b_pw1", j * 16 + co + KC)
                tiles = [(H, slice(t2 * TT, (t2 + 1) * TT), HALO + t2 * TT, TT) for t2 in range(2)]
                if hh == 0 and os.environ.get("DBG_B1", "") != "nohalo":
                    tiles.append((HH, slice(0, HALO), 0, HALO))
                for (src, ssl, uo, n) in tiles:
                    pa = self.psbank(self.next_bank())[0:n]
                    pg = self.psbank(self.next_bank())[0:n]
                    for (w3, pb) in ((wa3, pa), (wg3, pg)):
                        for k in range(KC):
                            S.op("pe", (lambda w3, pb, k, src, ssl: lambda e: e.matmul(
                                pb.ap, lhsT=w3[k].ap, rhs=src[k, ssl].ap,
                                start=(k == 0), stop=(k == KC - 1)))(w3, pb, k, src, ssl),
                                reads=[w3[k], src[k, ssl]], writes=[pb])
                    sg = SGT[cnt % 2][0:n]
                    cnt += 1
                    S.op("act", (lambda sg, pg, bg: lambda e: e.activation(
                        out=sg.ap, in_=pg.ap, func=AF.Sigmoid, bias=bg.ap))(sg, pg, bg),
                        reads=[pg, bg], writes=[sg])
                    if n == HALO and os.environ.get("DBG_B1", "") != "nohv":
                        S.op("dve", (lambda sg: lambda e: e.tensor_scalar(
                            out=sg.ap, in0=sg.ap, scalar1=hv.ap, scalar2=None, op0=ALU.mult))(sg),
                            reads=[sg, hv], writes=[sg])
                    ud = U[co, uo:uo + n]
                    S.op("dve", (lambda sg, pa, ba, ud: lambda e: e.scalar_tensor_tensor(
                        out=ud.ap, in0=pa.ap, scalar=ba.ap, in1=sg.ap, op0=ALU.add, op1=ALU.mult))(sg, pa, ba, ud),
                        reads=[pa, ba, sg], writes=[ud])
            if stop == "B1":
                continue
            stb_ids = [self.next_bank() for _ in range(4)]
            self.held = set(stb_ids)
            st_sum = [self.psbank(b) for b in stb_ids[0:2]]
            st_sq = [self.psbank(b) for b in stb_ids[2:4]]
            for co in range(KC):
                a0, _ = CL.sl("w_dw")
                c0 = a0 + (j * KC + co) * CW
                wcols = self.CST[c0:c0 + CW]
                DG = DGS[co % 2]
                S.I("dve", "tensor_tensor", reads=[self.IDENT.whole(), wcols], writes=[DG.whole()],
                    out=DG.whole().ap,
                    in0=self.IDENT.whole().ap.unsqueeze(1).to_broadcast([P, CW, P]),
                    in1=wcols.ap.unsqueeze(2).to_broadcast([P, CW, P]), op=ALU.mult)
                bd = self.cst_col("b_dw", j * KC + co)
                for t2 in range(2):
                    ls = slice(t2 * TT, (t2 + 1) * TT)
                    pc = self.psbank(self.next_bank())
                    for k in range(CW):
                        rr = U[co, t2 * TT + 2 + k: t2 * TT + 2 + k + TT]
                        S.I("pe", "matmul", reads=[DG[k], rr], writes=[pc.whole()],
                            out=pc.whole().ap, lhsT=DG[k].ap, rhs=rr.ap, start=(k == 0), stop=(k == CW - 1))
                    vd = V[co, ls]
                    S.op("act", (lambda vd, pc, bd: lambda e: e.activation(
                        out=vd.ap, in_=pc.whole().ap, func=AF.Identity, bias=bd.ap))(vd, pc, bd),
                        reads=[pc.whole(), bd], writes=[vd])
                    v16 = V16[cnt % 2]
                    q16 = Q16[cnt % 2]
                    cnt += 1
                    S.op("pool", (lambda v16, vd: lambda e: e.tensor_copy(out=v16.whole().ap, in_=vd.ap))(v16, vd),
                         reads=[vd], writes=[v16.whole()])
                    S.op("pool", (lambda q16, vd: lambda e: e.tensor_tensor(
                        out=q16.whole().ap, in0=vd.ap, in1=vd.ap, op=ALU.mult))(q16, vd),
                        reads=[vd], writes=[q16.whole()])
                    for (src16, stb) in ((v16, st_sum[t2]), (q16, st_sq[t2])):
                        S.op("pe", (lambda src16, stb, co: lambda e: e.matmul(
                            stb.whole().ap, lhsT=self.ONES.whole().ap, rhs=src16.whole().ap,
                            start=(co == 0), stop=(co == KC - 1)))(src16, stb, co),
                            reads=[self.ONES.whole(), src16.whole()], writes=[stb.whole()])
            self.held = set()
            if stop == "B2":
                continue
            for t2 in range(2):
                ls = slice(t2 * TT, (t2 + 1) * TT)
                rsa = RSA[t2]
                rs = RS[t2]
                S.op("dve", (lambda stb: lambda e: e.tensor_scalar(
                    out=MEAN.whole().ap, in0=stb.whole().ap, scalar1=1.0 / D, scalar2=None, op0=ALU.mult))(st_sum[t2]),
                    reads=[st_sum[t2].whole()], writes=[MEAN.whole()])
                S.op("dve", lambda e: e.tensor_tensor(out=MSQ.whole().ap, in0=MEAN.whole().ap, in1=MEAN.whole().ap,
                                                      op=ALU.mult),
                     reads=[MEAN.whole()], writes=[MSQ.whole()])
                S.op("dve", (lambda stb: lambda e: e.scalar_tensor_tensor(
                    out=MSQ.whole().ap, in0=stb.whole().ap, scalar=1.0 / D, in1=MSQ.whole().ap,
                    op0=ALU.mult, op1=ALU.subtract))(st_sq[t2]),
                    reads=[st_sq[t2].whole(), MSQ.whole()], writes=[MSQ.whole()])
                S.op("act", (lambda rsa: lambda e: e.activation(out=rsa.whole().ap, in_=MSQ.whole().ap, func=AF.Sqrt,
                                                                bias=self.EPSC.whole().ap))(rsa),
                     reads=[MSQ.whole(), self.EPSC.whole()], writes=[rsa.whole()])
                S.op("dve", (lambda rs, rsa: lambda e: e.reciprocal(out=rs.whole().ap, in_=rsa.whole().ap))(rs, rsa),
                     reads=[rsa.whole()], writes=[rs.whole()])
                for co in range(KC):
                    vd = V[co, ls]
                    lg = self.cst_col("ln_g", j * KC + co)
                    lb = self.cst_col("ln_b", j * KC + co)
                    S.I("pool", "tensor_tensor", reads=[vd, MEAN.whole()], writes=[vd],
                        out=vd.ap, in0=vd.ap, in1=MEAN.whole().ap, op=ALU.subtract)
                    S.op("dve", (lambda vd, rs: lambda e: e.tensor_tensor(
                        out=vd.ap, in0=vd.ap, in1=rs.whole().ap, op=ALU.mult))(vd, rs),
                        reads=[vd, rs.whole()], writes=[vd])
                    S.op("act", (lambda vd, co, ls, lg, lb: lambda e: e.activation(
                        out=H[co, ls].ap, in_=vd.ap, func=AF.Silu, bias=lb.ap, scale=lg.ap))(vd, co, ls, lg, lb),
                        reads=[vd, lg, lb], writes=[H[co, ls]])
            if stop == "C":
                continue
            w2 = [self.load_piece(wbase(16 + ci)) for ci in range(KC)]
            for t2 in range(2):
                ls = slice(t2 * TT, (t2 + 1) * TT)
                gs = slice(t0 + t2 * TT, t0 + (t2 + 1) * TT)
                for dc in range(KC):
                    pb = self.psbank(self.next_bank())
                    b2 = self.cst_col("b_pw2", j * KC + dc)
                    for ci in range(KC):
                        S.op("pe", (lambda ci, wsl, pb, ls: lambda e: e.matmul(
                            pb.whole().ap, lhsT=wsl.ap, rhs=H[ci, ls].ap,
                            start=(ci == 0), stop=(ci == KC - 1)))(ci, w2[ci][dc * P:(dc + 1) * P], pb, ls),
                            reads=[w2[ci][dc * P:(dc + 1) * P], H[ci, ls]], writes=[pb.whole()])
                    S.op("dve", (lambda pb, b2, dc, gs: lambda e: e.scalar_tensor_tensor(
                        out=self.X[dc, gs].ap, in0=pb.whole().ap, scalar=b2.ap, in1=self.X[dc, gs].ap,
                        op0=ALU.add, op1=ALU.add))(pb, b2, dc, gs),
                        reads=[pb.whole(), b2, self.X[dc, gs]], writes=[self.X[dc, gs]])


    def attn(self, li, mode="x"):
        S = self.S
        nc = self.nc
        self.ensure_eps()
        j = li // 2
        DIL = (1, 4, 16)
        xo = self.X.off
        Q = self.sbt(xo, BF16, (6, T))
        K = self.sbt(xo + 6 * T * 2, BF16, (6, T))
        DEN = self.sbt(xo + 12 * T * 2, F32, (2, T))
        cur = self.scratch()
        H = self.sbt(self.salloc(cur, KC * T * 2), BF16, (KC, T))
        O = H
        VT = self.sbt(self.salloc(cur, 6 * 16 * P * 2), BF16, (6, 16, P))
        o_halo = self.salloc(cur, 84 * P * 2)
        HAL = self.sbt(o_halo, BF16, (84, P))
        SQ = self.sbt(o_halo, BF16, (KC, TT))
        RSA = [self.sbt(o_halo + 8192 + i * 2048, F32, (TT,)) for i in range(2)]
        RS = [self.sbt(o_halo + 8192 + 4096 + i * 2048, F32, (TT,)) for i in range(2)]
        BIAS = self.sbt(self.salloc(cur, NH * 256 * 4), F32, (NH, 256))
        if mode == "prod":
            K = self.sbt(o_halo, BF16, (6, T))
            assert 6 * T * 2 <= 84 * P * 2 + NH * 256 * 4
        VTMP = [self.sbt(self.salloc(cur, T * 2), BF16, (T,)) for _ in range(2)]
        STMP = [self.sbt(self.salloc(cur, TT * 4), F32, (TT,)) for _ in range(2)]
        PT = [self.sbt(self.salloc(cur, TT * 2), BF16, (TT,)) for _ in range(3)]
        hb = self.cst_col("hb", 0)
        wbase = lambda pc: self.wattn[j, pc]
        xpark = nc.dram_tensor("xpark%d" % li, [P, KC, T], F32)
        park_sems = [self.new_sem("park%d_%d" % (li, k)) for k in range(KC)]

        if mode != "prod":
            S.I("sp", "dma_start", writes=[BIAS.whole()], dma_sem=self.new_sem("bias%d" % li),
                out=BIAS.whole().ap, in_=self.bmask.rearrange("p (h c) -> p h c", h=NH))
        for tt in range(NT):
            self.rmsnorm_tile(tt, "norm_mix", li, H, SQ, RSA[tt % 2], RS[tt % 2])
        for k in range(KC):
            if mode == "prod":
                break
            S.I("sp", "dma_start", reads=[self.X[k]], writes=[dref(None, "xpark%d" % li, [k])], dma_sem=park_sems[k],
                out=xpark[:, k, :], in_=self.X[k].ap)

        import os
        astop = os.environ.get("DBG_ATTN_STOP", "")
        if astop == "park":
            return
        evi = 0

        def project(oc, dst_row, scale):
            nonlocal evi
            g = (oc % 6) // 2
            d = DIL[g]
            w = self.load_piece(wbase(oc))
            w3 = Tn(self.sb, w.off, BF16, (KC, P))
            dst3 = Tn(self.sb, dst_row.off, BF16, (d, T // d))
            for tt in range(NT):
                ts = slice(tt * TT, (tt + 1) * TT)
                pb = self.psbank(self.next_bank())
                for k in range(KC):
                    S.I("pe", "matmul", reads=[w3[k], H[k, ts]], writes=[pb.whole()],
                        out=pb.whole().ap, lhsT=w3[k].ap, rhs=H[k, ts].ap, start=(k == 0), stop=(k == KC - 1))
                n = TT // d
                dref_ = dst3[:, tt * n:(tt + 1) * n]
                src_ap = pb.whole().ap
                out_ap = dref_.ap
                if d > 1:
                    src_ap = src_ap.rearrange("p (j r) -> p r j", r=d)
                else:
                    out_ap = dst_row[tt * TT:(tt + 1) * TT].ap
                if evi % 2 == 0:
                    S.I("act", "activation", reads=[pb.whole()], writes=[dref_], nowaw=True,
                        out=out_ap, in_=src_ap, func=AF.Copy, scale=scale)
                else:
                    S.I("dve", "tensor_scalar", reads=[pb.whole()], writes=[dref_], nowaw=True,
                        out=out_ap, in0=src_ap, scalar1=scale, scalar2=None, op0=ALU.mult)
                evi += 1

        tbank = None
        for vc in range(6):
            vt = VTMP[vc % 2]
            project(12 + vc, vt, 1.0)
            for u0 in (0, 8):
                pb = self.psbank(self.next_bank(), BF16)
                for i in range(8):
                    u = u0 + i
                    S.I("pe", "transpose", reads=[vt[u * P:(u + 1) * P], self.IDENT.whole()],
                        writes=[pb[i * P:(i + 1) * P]],
                        out=pb[i * P:(i + 1) * P].ap, in_=vt[u * P:(u + 1) * P].ap, identity=self.IDENT.whole().ap)
                S.I("dve" if u0 == 0 else "act", "tensor_copy" if u0 == 0 else "copy",
                    reads=[pb.whole()], writes=[VT[vc, u0:u0 + 8]],
                    out=VT[vc, u0:u0 + 8].ap, in_=pb.whole().ap.rearrange("p (a b) -> p a b", a=8))
        if astop == "v":
            return
        for kc in range(6):
            project(6 + kc, Tn(self.sb, K.off + kc * T * 2, BF16, (T,)), 1.0)
        if astop == "k":
            return

        NB = 84
        bk0 = {}
        idx = 0
        for kc in range(6):
            bk0[kc] = idx
            idx += DIL[kc // 2]
        assert idx == 42
        if mode == "x":
            bnc_t = nc.dram_tensor("bnc_at%d" % li, [P, NB, P], BF16)
            bnc = bnc_t
        elif mode == "prod":
            bnc = self.halo_out
        if mode in ("x", "prod"):
            bsem = self.new_sem("bat%d" % li)
            sends = []
            for kc in range(6):
                d = DIL[kc // 2]
                per = 16 // d
                kv = Tn(self.sb, K.off + kc * T * 2, BF16, (d, per, P))
                src_k = kv[:, per - 1, :]
                src_v_ap = VT.ap[:, kc, per - 1::per, :]
                src_v = Ref(src_v_ap, "sb", VT[kc].gr)
                for (src, b0) in ((src_k, bk0[kc]), (src_v, 42 + bk0[kc])):
                    sends.append(S.I("sp", "dma_start", reads=[src], writes=[dref(None, "bnc_at%d" % li, [b0])],
                                     dma_sem=bsem, out=bnc[:, b0:b0 + d, :], in_=src.ap))
        if mode == "prod":
            self.tail_ops += sends
            return
        if mode == "x":
            NCH = int(os.environ.get("ATT_NCH", "7"))
            CB = NB // NCH
            for ch in range(NCH):
                bch = nc.dram_tensor("bch_at%d_%d" % (li, ch), [P, CB * P], BF16)
                gch = nc.dram_tensor("gch_at%d_%d" % (li, ch), [NCORES * P, CB * P], BF16)
                nm = "at%d_%d" % (li, ch)
                S.I("pool", "dma_start", reads=[dref(None, "bnc_at%d" % li, list(range(84)))],
                    writes=[dref(None, "bch_" + nm, [0])], dma_sem=self.new_sem("c1" + nm),
                    out=bch[:, :].rearrange("p (a b) -> p a b", a=CB), in_=bnc_t[:, ch * CB:(ch + 1) * CB, :])
                S.I("pool", "collective_compute", reads=[dref(None, "bch_" + nm, [0])],
                    writes=[dref(None, "gch_" + nm, [0])], dma_sem=self.new_sem("c2" + nm), inc=1,
                    kind="AllGather", op=ALU.bypass, replica_groups=[list(range(NCORES))],
                    ins=[bch.ap().opt()], outs=[gch.ap().opt()])

                def ld(e, gch=gch, ch=ch, CB=CB):
                    prev = self.prev_rank()
                    return e.dma_start(out=HAL[ch * CB:(ch + 1) * CB].ap,
                                       in_=gch[bass.ds(prev * P, P), :].rearrange("p (a b) -> p a b", a=CB))
                S.op("pool", ld, reads=[dref(None, "gch_" + nm, [0])], writes=[HAL[ch * CB:(ch + 1) * CB]],
                     dma_sem=self.new_sem("c3" + nm))
        else:
            S.I("sp", "dma_start", writes=[HAL.whole()], dma_sem=self.new_sem("hat%d" % li),
                out=HAL.whole().ap, in_=self.halo_in)

        if astop == "x":
            return
        for qc in range(6):
            project(qc, Tn(self.sb, Q.off + qc * T * 2, BF16, (T,)), 0.125)
        if astop == "q":
            return

        pti = 0
        sti = 0
        kcs = [int(c) for c in os.environ.get("DBG_KCS", "012345")]
        S.I("dve", "memset", writes=[DEN.whole()], ap=DEN.whole().ap, constant=0.0)
        for (halo_pass, kc) in [(hp_, k_) for hp_ in (0, 1) for k_ in kcs]:
            g = kc // 2
            d = DIL[g]
            per = 16 // d
            cls = kc % 2
            Krow = Tn(self.sb, K.off + kc * T * 2, BF16, (16, P))
            Qrow = Tn(self.sb, Q.off + kc * T * 2, BF16, (16, P))
            o_row = O.ap[:, kc, :]
            den_row = DEN.ap[:, cls, :]
            for up in range(8):
                if (((2 * up) % per == 0) or ((2 * up + 1) % per == 0)) != bool(halo_pass):
                    continue
                od = self.psbank(self.next_bank())
                dd = self.psbank(self.next_bank())
                units = (2 * up, 2 * up + 1)
                for hp in range(2):
                    h = 2 * kc + hp
                    ps_ = slice(hp * 64, (hp + 1) * 64)
                    sb_ = self.psbank(self.next_bank())
                    halo_unit = []
                    for ui, u in enumerate(units):
                        nb = u % per
                        r = u // per
                        if nb == 0:
                            kprev = HAL.pslice(hp * 64, (hp + 1) * 64, (bk0[kc] + r,))
                        else:
                            kprev = Krow.pslice(hp * 64, (hp + 1) * 64, (u - 1,))
                        halo_unit.append(nb == 0)
                        kcur = Krow.pslice(hp * 64, (hp + 1) * 64, (u,))
                        qq = Qrow.pslice(hp * 64, (hp + 1) * 64, (u,))
                        for bi, kk in enumerate((kprev, kcur)):
                            c0 = ui * 256 + bi * 128
                            S.I("pe", "matmul", reads=[kk, qq], writes=[sb_[c0:c0 + P]],
                                out=sb_[c0:c0 + P].ap, lhsT=kk.ap, rhs=qq.ap, start=True, stop=True)
                    st = STMP[sti % 2]
                    sti += 1
                    G = BIAS[h]
                    for ui, u in enumerate(units):
                        c0 = ui * 256
                        if halo_unit[ui]:
                            S.I("dve", "scalar_tensor_tensor", reads=[sb_[c0:c0 + P], hb, BIAS[h, 0:P]],
                                writes=[st[c0:c0 + P]],
                                out=st[c0:c0 + P].ap, in0=sb_[c0:c0 + P].ap, scalar=hb.ap, in1=BIAS[h, 0:P].ap,
                                op0=ALU.add, op1=ALU.add)
                            S.I("dve", "tensor_tensor", reads=[sb_[c0 + P:c0 + 256], BIAS[h, P:256]],
                                writes=[st[c0 + P:c0 + 256]],
                                out=st[c0 + P:c0 + 256].ap, in0=sb_[c0 + P:c0 + 256].ap, in1=BIAS[h, P:256].ap,
                                op=ALU.add)
                        else:
                            S.I("dve", "tensor_tensor", reads=[sb_[c0:c0 + 256], G], writes=[st[c0:c0 + 256]],
                                out=st[c0:c0 + 256].ap, in0=sb_[c0:c0 + 256].ap, in1=G.ap, op=ALU.add)
                    pt = PT[pti % 3]
                    pti += 1
                    lstop = os.environ.get("DBG_LOOP", "")
                    if lstop == "s":
                        continue
                    S.I("act", "activation", reads=[st.whole()], writes=[pt.whole()],
                        out=pt.whole().ap, in_=st.whole().ap, func=AF.Exp)
                    if lstop == "e":
                        continue
                    for ui, u in enumerate(units):
                        nb = u % per
                        r = u // per
                        if nb == 0:
                            vprev = HAL[42 + bk0[kc] + r, hp * 64:(hp + 1) * 64]
                        else:
                            vprev = VT[kc, u - 1, hp * 64:(hp + 1) * 64]
                        vcur = VT[kc, u, hp * 64:(hp + 1) * 64]
                        oc0 = ui * P
                        o_out = od.pslice(hp * 64, (hp + 1) * 64, (slice(oc0, oc0 + P),))
                        d_out = dd.pslice(hp * 64, (hp + 1) * 64, (slice(oc0, oc0 + P),))
                        for bi, vv in enumerate((vprev, vcur)):
                            pblk = pt[ui * 256 + bi * P: ui * 256 + (bi + 1) * P]
                            S.I("pe", "matmul", reads=[vv, pblk], writes=[o_out],
                                out=o_out.ap, lhsT=vv.ap, rhs=pblk.ap, start=(bi == 0), stop=(bi == 1))
                        for bi in range(2):
                            pblk = pt[ui * 256 + bi * P: ui * 256 + (bi + 1) * P]
                            S.I("pe", "matmul", reads=[self.ONES[0:64], pblk], writes=[d_out],
                                out=d_out.ap, lhsT=self.ONES[0:64].ap, rhs=pblk.ap, start=(bi == 0), stop=(bi == 1))
                if os.environ.get("DBG_LOOP", "") in ("s", "e", "o"):
                    continue
                od3 = od[0:2 * P].ap.rearrange("p (u c) -> p u c", u=2)
                dd3 = dd[0:2 * P].ap.rearrange("p (u c) -> p u c", u=2)
                u = units[0]
                nb = u % per
                r = u // per
                if d == 1:
                    o_dst = o_row.rearrange("p (a b) -> p a b", b=P)[:, u:u + 2, :]
                    d_dst = den_row.rearrange("p (a b) -> p a b", b=P)[:, u:u + 2, :]
                elif d == 4:
                    o_dst = o_row.rearrange("p (a b r) -> p r a b", r=4, b=P)[:, r, nb:nb + 2, :]
                    d_dst = den_row.rearrange("p (a b r) -> p r a b", r=4, b=P)[:, r, nb:nb + 2, :]
                else:
                    o_dst = o_row.rearrange("p (b r) -> p r b", r=16)[:, r:r + 2, :]
                    d_dst = den_row.rearrange("p (b r) -> p r b", r=16)[:, r:r + 2, :]
                o_ref = Ref(o_dst, "sb", O[kc].gr)
                d_ref = Ref(d_dst, "sb", DEN[cls].gr)
                S.I("act", "activation", reads=[od.whole()], writes=[o_ref], nowaw=True,
                    out=o_dst, in_=od3, func=AF.Copy)
                S.I("dve", "tensor_tensor", reads=[dd.whole(), d_ref], writes=[d_ref],
                    out=d_dst, in0=dd3, in1=d_dst, op=ALU.add)

        if astop == "a":
            return
        for cls in range(2):
            for tt in range(NT):
                ts = slice(tt * TT, (tt + 1) * TT)
                S.I("dve", "reciprocal", reads=[DEN[cls, ts]], writes=[DEN[cls, ts]],
                    out=DEN[cls, ts].ap, in_=DEN[cls, ts].ap)
        for tt in range(NT):
            ts = slice(tt * TT, (tt + 1) * TT)
            for kc in range(6):
                S.I("dve", "tensor_tensor", reads=[O[kc, ts], DEN[kc % 2, ts]], writes=[O[kc, ts]],
                    out=O[kc, ts].ap, in0=O[kc, ts].ap, in1=DEN[kc % 2, ts].ap, op=ALU.mult)
        for k in range(KC):
            S.I("sp", "dma_start", reads=[dref(None, "xpark%d" % li, [k])], writes=[self.X[k]], dma_sem=park_sems[k],
                out=self.X[k].ap, in_=xpark[:, k, :])
        wo = [self.load_piece(wbase(18 + ci)) for ci in range(6)]
        for tt in range(NT):
            ts = slice(tt * TT, (tt + 1) * TT)
            for dc in range(KC):
                pb = self.psbank(self.next_bank())
                for ci in range(6):
                    wsl = wo[ci][dc * P:(dc + 1) * P]
                    S.I("pe", "matmul", reads=[wsl, O[ci, ts]], writes=[pb.whole()],
                        out=pb.whole().ap, lhsT=wsl.ap, rhs=O[ci, ts].ap, start=(ci == 0), stop=(ci == 5))
                S.I("dve", "tensor_tensor", reads=[pb.whole(), self.X[dc, ts]], writes=[self.X[dc, ts]],
                    out=self.X[dc, ts].ap, in0=pb.whole().ap, in1=self.X[dc, ts].ap, op=ALU.add)


RUN_KW = {}
LAST_RES = None
ALL_STAGES = [("conv", 0), ("ffn", 0), ("attn", 1), ("ffn", 1), ("conv", 2), ("ffn", 2), ("attn", 3), ("ffn", 3)]


def t5_bucket_np(dist):
    n_buckets, max_distance = 32, 2048
    max_exact = n_buckets // 2
    n = np.maximum(dist, 0)
    nf = np.maximum(n, 1).astype(np.float32)
    large = max_exact + (np.log(nf / np.float32(max_exact)) / np.float32(np.log(max_distance / max_exact))
                         * np.float32(n_buckets - max_exact)).astype(np.int32)
    large = np.minimum(large, n_buckets - 1)
    return np.where(n < max_exact, n, large)


def bias_tables(rel_bias):
    out = np.full((P, NH, 256), NEG, np.float32)
    jj = np.arange(P)[:, None]
    ii = np.arange(P)[None, :]
    for h in range(NH):
        d = (1, 4, 16)[h // 4]
        dist_prev = ii + P - jj
        dist_cur = ii - jj
        bp = rel_bias[t5_bucket_np(dist_prev * d), h]
        bc = rel_bias[t5_bucket_np(dist_cur * d), h]
        out[:, h, 0:P] = np.where(jj >= ii, bp, NEG)
        out[:, h, P:256] = np.where(jj <= ii, bc, NEG)
    return np.ascontiguousarray(out.reshape(P, NH * 256))


def prep_common(inputs):
    c = {}
    g = lambda k: np.asarray(inputs[k], np.float32)
    cst = np.zeros((P, CL.n), np.float32)

    def put(name, arr):
        a, b = CL.sl(name)
        assert arr.shape == (P, b - a), (name, arr.shape, b - a)
        cst[:, a:b] = arr

    put("norm_mix", colmajor(g("norm_mix"), KC))
    put("norm_ffn", colmajor(g("norm_ffn"), KC))
    put("final_norm", colmajor(g("final_norm"), KC))
    put("b_pw1", colmajor(g("conv_b_pw1"), 16))
    wdw = g("conv_w_dw").reshape(2, CW, KC, P).transpose(3, 0, 2, 1).reshape(P, -1)
    put("w_dw", np.ascontiguousarray(wdw))
    put("b_dw", colmajor(g("conv_b_dw"), KC))
    put("ln_g", colmajor(g("conv_ln_g"), KC))
    put("ln_b", colmajor(g("conv_ln_b"), KC))
    put("b_pw2", colmajor(g("conv_b_pw2"), KC))
    c["cst"] = cst
    wg = g("ffn_w_gate").reshape(DEPTH, KC, P, FC, P).transpose(0, 3, 2, 1, 4).reshape(DEPTH, FC, P, 1024)
    wu = g("ffn_w_up").reshape(DEPTH, KC, P, FC, P).transpose(0, 3, 2, 1, 4).reshape(DEPTH, FC, P, 1024)
    wd = g("ffn_w_down").reshape(DEPTH, FC, P, D)
    c["wffn"] = np.ascontiguousarray(np.stack([wg, wu, wd], axis=1))
    w1 = g("conv_w_pw1").reshape(2, KC, P, 16, P).transpose(0, 3, 2, 1, 4).reshape(2, 16, P, 1024)
    w2 = g("conv_w_pw2").reshape(2, KC, P, D)
    c["wconv"] = np.ascontiguousarray(np.concatenate([w1, w2], axis=1))
    c["identd"] = np.eye(P, dtype=np.float32)
    wq = g("attn_w_qkv").reshape(2, KC, P, 18, P).transpose(0, 3, 2, 1, 4).reshape(2, 18, P, 1024)
    wo = g("attn_w_o").reshape(2, 6, P, D)
    c["wattn"] = np.ascontiguousarray(np.concatenate([wq, wo], axis=1))
    c["bmask"] = bias_tables(g("rel_bias"))
    return c


def x_to_cores(x):
    xf = np.asarray(x, np.float32).reshape(NCORES, T, KC, P)
    return [np.ascontiguousarray(xf[c].transpose(2, 1, 0)) for c in range(NCORES)]


def cores_to_x(ys, B, S):
    out = np.stack([y.transpose(2, 1, 0).reshape(T, D) for y in ys], axis=0)
    return np.ascontiguousarray(out.reshape(B, S, D)).astype(np.float32)


def launch(common, xs, stages, final, halo_in=None):
    b = Builder(stages, final)
    nc = b.build()
    kinds = [k for (k, _) in stages]
    in_maps = []
    for c in range(NCORES):
        m = {"xT": xs[c]}
        cst = common["cst"].copy()
        a, _ = CL.sl("hv")
        cst[:, a] = 0.0 if c % 4 == 0 else 1.0
        a, _ = CL.sl("hb")
        cst[:, a] = NEG if c % 4 == 0 else 0.0
        m["cst"] = cst
        for k in ("wffn", "wconv", "identd", "wattn", "bmask"):
            m[k] = common[k]
        if "attn_in" in kinds:
            m["halo_in"] = halo_in[(c - 1) % NCORES]
        if "conv_in" in kinds:
            m["xh_in"] = np.ascontiguousarray(xs[(c - 1) % NCORES][:, :, T - HALO:])
        in_maps.append(m)
    res = run_bass_kernel_spmd(nc, in_maps, core_ids=list(range(NCORES)), **RUN_KW)
    global LAST_RES
    LAST_RES = res
    ys = [r["yT"] for r in res.results]
    ho = [r["halo_out"] for r in res.results] if "attn_prod" in kinds else None
    return ys, ho


def run_stages(inputs, x, stages, final):
    common = prep_common(inputs)
    ys, _ = launch(common, x_to_cores(x), stages, final)
    B_, S_ = np.asarray(inputs["x"]).shape[:2]
    return cores_to_x(ys, B_, S_)


def kernel_unfused(inputs):
    common = prep_common(inputs)
    xs = x_to_cores(inputs["x"])
    xs, h1 = launch(common, xs, [("conv_in", 0), ("ffn", 0), ("attn_prod", 1)], False)
    xs, _ = launch(common, xs, [("attn_in", 1), ("ffn", 1)], False, h1)
    xs, h3 = launch(common, xs, [("conv_in", 2), ("ffn", 2), ("attn_prod", 3)], False)
    ys, _ = launch(common, xs, [("attn_in", 3), ("ffn", 3)], True, h3)
    B_, S_ = np.asarray(inputs["x"]).shape[:2]
    return cores_to_x(ys, B_, S_)


FUSED = False


def kernel(**inputs):
    if FUSED:
        return run_stages(inputs, inputs["x"], ALL_STAGES, True)
    return kernel_unfused(inputs)
```
